# Optimizing a Trainium2 kernel written in Bass

```python
import jax
import jax.numpy as jnp
from jax import lax
import numpy as np

D_MODEL = 1024
BATCH = 8
SEQ = 4096
DEPTH = 1

ATT_HEADS = 8
ATT_KV_HEADS = 2
ATT_GROUP = ATT_HEADS // ATT_KV_HEADS
ATT_HEAD_DIM = 64
WINDOW = 128
ATT_BLOCK = 128
ROPE_DIM = ATT_HEAD_DIM // 4
ROPE_THETA = 500000.0

RET_HEADS = 4
RET_KEY_DIM = 128
RET_VAL_DIM = 256
RET_CHUNK = 128
RET_ROT_BASE = 10000.0

D_FF = 4 * D_MODEL

NORM_EPS = 1e-6
GN_EPS = 1e-6
NEG_INF = -1e30

ATT_Q_W = ATT_HEADS * ATT_HEAD_DIM
ATT_KV_W = ATT_KV_HEADS * ATT_HEAD_DIM
RET_QK_W = RET_HEADS * RET_KEY_DIM
RET_V_W = RET_HEADS * RET_VAL_DIM
IN_SPLITS = (ATT_Q_W, ATT_KV_W, ATT_KV_W, RET_QK_W, RET_QK_W, RET_V_W, RET_V_W, D_MODEL, D_MODEL)
IN_WIDTH = sum(IN_SPLITS)
IN_OFFSETS = tuple(int(v) for v in np.cumsum(IN_SPLITS)[:-1])

kernel_name = "hybrid_swa_sink_retention_gated_block"


def rmsnorm(x, gain):
    xf = x.astype(jnp.float32)
    y = xf * lax.rsqrt(jnp.mean(xf * xf, axis=-1, keepdims=True) + NORM_EPS)
    return (y * gain.astype(jnp.float32)).astype(x.dtype)


def rope_tables(seq_len, dim, theta):
    pos = jnp.arange(seq_len, dtype=jnp.float32)
    inv_freq = theta ** (-jnp.arange(0, dim, 2, dtype=jnp.float32) / dim)
    ang = pos[:, None] * inv_freq[None, :]
    return jnp.cos(ang)[:, None, :], jnp.sin(ang)[:, None, :]


def apply_rope(x, cos, sin):
    xf = x.astype(jnp.float32)
    x1, x2 = jnp.split(xf, 2, axis=-1)
    out = jnp.concatenate([x1 * cos - x2 * sin, x2 * cos + x1 * sin], axis=-1)
    return out.astype(x.dtype)


def partial_rope(x, cos, sin):
    return jnp.concatenate([apply_rope(x[..., :ROPE_DIM], cos, sin), x[..., ROPE_DIM:]], axis=-1)


def sliding_window_sink_attention(q, k, v, sinks):
    b, s, _, dh = q.shape
    c = ATT_BLOCK
    nb = s // c
    qb = q.reshape(b, nb, c, ATT_KV_HEADS, ATT_GROUP, dh)
    kb = k.reshape(b, nb, c, ATT_KV_HEADS, dh)
    vb = v.reshape(b, nb, c, ATT_KV_HEADS, dh)

    def with_prev(t):
        prev = jnp.concatenate([jnp.zeros_like(t[:, :1]), t[:, :-1]], axis=1)
        return jnp.concatenate([prev, t], axis=2)

    kw, vw = with_prev(kb), with_prev(vb)
    scores = jnp.einsum('bnqhgd,bnkhd->bnhgqk', qb, kw,
                        preferred_element_type=jnp.float32) * (dh ** -0.5)
    qi = jnp.arange(c)[:, None] + c
    kj = jnp.arange(2 * c)[None, :]
    delta = qi - kj
    in_window = (delta >= 0) & (delta < WINDOW)
    has_prev = (jnp.arange(nb) > 0)[:, None, None] | (kj >= c)[None]
    mask = in_window[None] & has_prev
    scores = jnp.where(mask[None, :, None, None], scores, NEG_INF)
    sink = sinks.astype(jnp.float32).reshape(ATT_KV_HEADS, ATT_GROUP)[None, None, :, :, None, None]
    sink = jnp.broadcast_to(sink, scores.shape[:-1] + (1,))
    probs = jax.nn.softmax(jnp.concatenate([scores, sink], axis=-1), axis=-1)[..., :-1]
    out = jnp.einsum('bnhgqk,bnkhd->bnqhgd', probs.astype(v.dtype), vw)
    return out.reshape(b, s, ATT_HEADS * dh)


def chunkwise_retention(q, k, v):
    b, s, h, dk = q.shape
    dv = v.shape[-1]
    c = RET_CHUNK
    nc = s // c
    log_gamma = jnp.log1p(-jnp.exp2(-5.0 - jnp.arange(h, dtype=jnp.float32)))
    idx = jnp.arange(c, dtype=jnp.float32)
    diff = idx[:, None] - idx[None, :]
    intra = jnp.where(diff >= 0, jnp.exp(jnp.maximum(diff, 0.0) * log_gamma[:, None, None]), 0.0)
    q_decay = jnp.exp((idx + 1.0)[None, :] * log_gamma[:, None])[..., None]
    k_decay = jnp.exp((c - 1.0 - idx)[None, :] * log_gamma[:, None])[..., None]
    chunk_decay = jnp.exp(c * log_gamma)[:, None, None]

    def to_chunks(t):
        return t.astype(jnp.float32).reshape(b, nc, c, h, t.shape[-1]).transpose(1, 0, 3, 2, 4)

    qc, kc, vc = to_chunks(q), to_chunks(k), to_chunks(v)

    def step(state, inp):
        qi, ki, vi = inp
        att = jnp.einsum('bhqd,bhkd->bhqk', qi, ki) * intra
        inner = jnp.einsum('bhqk,bhkv->bhqv', att, vi)
        cross = jnp.einsum('bhqd,bhdv->bhqv', qi * q_decay, state)
        new_state = state * chunk_decay + jnp.einsum('bhkd,bhkv->bhdv', ki * k_decay, vi)
        return new_state, inner + cross

    state0 = jnp.zeros((b, h, dk, dv), jnp.float32)
    _, out = lax.scan(step, state0, (qc, kc, vc))
    return out.transpose(1, 0, 3, 2, 4).reshape(b, s, h, dv)


def head_groupnorm(y, gain):
    b, s, h, dv = y.shape
    mu = jnp.mean(y, axis=-1, keepdims=True)
    var = jnp.mean(jnp.square(y - mu), axis=-1, keepdims=True)
    yn = (y - mu) * lax.rsqrt(var + GN_EPS)
    return yn.reshape(b, s, h * dv) * gain.astype(jnp.float32)


def mixer_block(xn, w_in, b_gates, attn_sinks, ret_gn_gain, w_att_up, w_ret_up, w_out, att_rope, ret_rope):
    b, s, _ = xn.shape
    proj = jnp.einsum('bsd,dn->bsn', xn, w_in)
    q_a, k_a, v_a, q_r, k_r, v_r, g_r, gate_a, gate_r = jnp.split(proj, IN_OFFSETS, axis=-1)

    q_a = partial_rope(q_a.reshape(b, s, ATT_HEADS, ATT_HEAD_DIM), *att_rope)
    k_a = partial_rope(k_a.reshape(b, s, ATT_KV_HEADS, ATT_HEAD_DIM), *att_rope)
    v_a = v_a.reshape(b, s, ATT_KV_HEADS, ATT_HEAD_DIM)
    y_a = sliding_window_sink_attention(q_a, k_a, v_a, attn_sinks) @ w_att_up

    q_r = apply_rope(q_r.reshape(b, s, RET_HEADS, RET_KEY_DIM), *ret_rope)
    k_r = apply_rope(k_r.reshape(b, s, RET_HEADS, RET_KEY_DIM), *ret_rope) * (RET_KEY_DIM ** -0.5)
    ret = chunkwise_retention(q_r, k_r, v_r.reshape(b, s, RET_HEADS, RET_VAL_DIM))
    ret = head_groupnorm(ret, ret_gn_gain)
    y_r = (jax.nn.silu(g_r.astype(jnp.float32)) * ret).astype(xn.dtype) @ w_ret_up

    bg = b_gates.astype(jnp.float32)
    ga = jax.nn.sigmoid(gate_a.astype(jnp.float32) + bg[:D_MODEL])
    gr = jax.nn.sigmoid(gate_r.astype(jnp.float32) + bg[D_MODEL:])
    merged = (ga * y_a.astype(jnp.float32) + gr * y_r.astype(jnp.float32)).astype(xn.dtype)
    return merged @ w_out


def squared_relu_mlp(xn, w_ff1, w_ff2):
    hdn = jnp.square(jax.nn.relu(xn @ w_ff1))
    return hdn @ w_ff2


def setup_inputs(seed: int = 0) -> dict:
    key = jax.random.key(seed)
    ks = jax.random.split(key, 14)
    f32 = jnp.float32

    def normal(k, shape, scale):
        return jax.random.normal(k, shape, f32) * scale

    def gain(k, shape):
        return 1.0 + 0.02 * jax.random.normal(k, shape, f32)

    return {
        'x': normal(ks[0], (BATCH, SEQ, D_MODEL), 1.0),
        'norm_mix_gain': gain(ks[1], (DEPTH, D_MODEL)),
        'w_in': normal(ks[2], (DEPTH, D_MODEL, IN_WIDTH), D_MODEL ** -0.5),
        'b_gates': normal(ks[3], (DEPTH, 2 * D_MODEL), 0.1),
        'attn_sinks': normal(ks[4], (DEPTH, ATT_HEADS), 0.5),
        'ret_gn_gain': gain(ks[5], (DEPTH, RET_V_W)),
        'w_att_up': normal(ks[6], (DEPTH, ATT_Q_W, D_MODEL), ATT_Q_W ** -0.5),
        'w_ret_up': normal(ks[7], (DEPTH, RET_V_W, D_MODEL), RET_V_W ** -0.5),
        'w_out': normal(ks[8], (DEPTH, D_MODEL, D_MODEL), D_MODEL ** -0.5),
        'norm_mlp_gain': gain(ks[9], (DEPTH, D_MODEL)),
        'w_ff1': normal(ks[10], (DEPTH, D_MODEL, D_FF), D_MODEL ** -0.5),
        'w_ff2': normal(ks[11], (DEPTH, D_FF, D_MODEL), D_FF ** -0.5),
        'norm_final_gain': gain(ks[12], (D_MODEL,)),
    }


def reference(x, norm_mix_gain, w_in, b_gates, attn_sinks, ret_gn_gain, w_att_up, w_ret_up, w_out,
              norm_mlp_gain, w_ff1, w_ff2, norm_final_gain):
    s = x.shape[1]
    att_rope = rope_tables(s, ROPE_DIM, ROPE_THETA)
    ret_rope = rope_tables(s, RET_KEY_DIM, RET_ROT_BASE)
    h = x
    for l in range(DEPTH):
        h = h + mixer_block(rmsnorm(h, norm_mix_gain[l]), w_in[l], b_gates[l], attn_sinks[l],
                            ret_gn_gain[l], w_att_up[l], w_ret_up[l], w_out[l], att_rope, ret_rope)
        h = h + squared_relu_mlp(rmsnorm(h, norm_mlp_gain[l]), w_ff1[l], w_ff2[l])
    return rmsnorm(h, norm_final_gain)
```

```python
import os
import numpy as np
import ml_dtypes
from contextlib import ExitStack
import concourse.bass as bass
import concourse.mybir as mybir
from concourse.bass_utils import run_bass_kernel_spmd

F32 = mybir.dt.float32
BF16 = mybir.dt.bfloat16
AF = mybir.ActivationFunctionType
ALU = mybir.AluOpType

D = 1024
NCORES = 8
SEQ = 4096
AH, AKV, ADH = 8, 2, 64
RH, RDK, RDV = 4, 128, 256
DFF = 4096
INW = 5888
OFF_QA, OFF_KA, OFF_VA, OFF_QR, OFF_KR, OFF_VR, OFF_GR, OFF_GA, OFF_GRT = 0, 512, 640, 768, 1280, 1792, 2816, 3840, 4864
EPS = 1e-6


class Op:
    __slots__ = ("eng", "fn", "deps", "signal", "sig_idx", "dma_sem", "dma_val", "is_dma", "waits", "name", "preds", "idx", "dur")

    def __init__(self, eng, fn, name=""):
        self.eng = eng
        self.fn = fn
        self.deps = []
        self.signal = False
        self.sig_idx = None
        self.dma_sem = None
        self.dma_val = None
        self.is_dma = False
        self.waits = []
        self.name = name
        self.preds = []
        self.idx = 0
        self.dur = 0.0


class Sched:
    ENGS = ("pe", "act", "dve", "pool", "sp")

    def __init__(self, same_engine_sync=True):
        self.ops = []
        self.last_writer = {}
        self.readers = {}
        self.dma_counts = {}
        self.same_engine_sync = same_engine_sync
        self.last_on = {}
        self.thr = {}
        self.MAXQ = 4
        self.do_reorder = True
        self.fences = []

    def _deps(self, op, reads, writes):
        psr = [r for r in reads if isinstance(r, tuple) and r[0] == "ps"]
        if psr:
            reads = [r for r in reads if r not in psr]
            writes = list(writes) + [r for r in psr if r not in writes]
        deps = []
        for r in reads:
            w = self.last_writer.get(r)
            if w is not None:
                deps.append(w)
        for w_ in writes:
            w = self.last_writer.get(w_)
            if w is not None:
                deps.append(w)
            deps.extend(self.readers.get(w_, ()))
        for r in reads:
            self.readers.setdefault(r, []).append(op)
        for w_ in writes:
            self.last_writer[w_] = op
            self.readers[w_] = []
        seen = set()
        for d in deps:
            if d is op or id(d) in seen:
                continue
            seen.add(id(d))
            op.preds.append(d)
            if d.eng == op.eng and not d.is_dma:
                if op.eng == "pe" or not self.same_engine_sync:
                    continue
            op.deps.append(d)

    def add(self, eng, fn, reads=(), writes=(), name=""):
        op = Op(eng, fn, name)
        self._deps(op, reads, writes)
        self.ops.append(op)
        self.last_on[eng] = op
        return op

    def dma(self, eng, fn, semkey, reads=(), writes=(), name="", throttle=None):
        op = Op(eng, fn, name)
        op.is_dma = True
        n = self.dma_counts.get(semkey, 0) + 1
        self.dma_counts[semkey] = n
        op.dma_sem = semkey
        op.dma_val = 16 * n
        self._deps(op, reads, writes)
        tcls = "pool" if eng == "pool" else throttle
        if tcls is not None:
            lst = self.thr.setdefault(tcls, [])
            lst.append(op)
            if len(lst) > self.MAXQ:
                op.deps.append(lst[-1 - self.MAXQ])
                op.preds.append(lst[-1 - self.MAXQ])
        self.ops.append(op)
        return op

    def barrier(self, eng, deps, name="barrier"):
        op = Op(eng, None, name)
        op.deps = [d for d in deps if d is not None]
        op.preds = list(op.deps)
        self.ops.append(op)
        return op

    def fence_all(self, extra=()):
        self.fences.append(len(self.ops))
        for e in self.ENGS:
            op = self.barrier(e, list(extra), name="fence")

    def _patch_fence(self, before, fence_ops):
        last = {}
        for o in before:
            if o.fn is not None and not o.is_dma:
                last[o.eng] = o
        for f in fence_ops:
            extra = [d for d in f.deps if d.is_dma]
            f.deps = extra + [d for e, d in last.items() if e != f.eng]
            f.preds = list(f.deps)

    def _est(self, op):
        if op.fn is None:
            return 0.05
        n = getattr(op.fn, "n", 256)
        if op.is_dma:
            return 2.0 + n * 128 * 4 / 150e3
        if op.eng == "pe":
            return 0.04 + 0.000315 * max(n, 64)
        if op.eng == "act":
            return 0.2 + n * 0.95e-3
        if op.eng == "dve":
            return 0.12 + n * 1.1e-3
        if op.eng == "pool":
            return 0.35 + n * 2.0e-3
        return 0.05

    def reorder(self, window=128, slack=0.05, segments=None):
        window = int(os.environ.get("SCH_W", window))
        slack = float(os.environ.get("SCH_S", slack))
        xlat = float(os.environ.get("SCH_L", 0.25))
        ops = self.ops
        for i, op in enumerate(ops):
            op.idx = i
            op.dur = self._est(op)
        bounds = [0] + sorted(segments or []) + [len(ops)]
        new_ops = []
        finish = {}
        eng_free = {e: 0.0 for e in self.ENGS}
        for si in range(len(bounds) - 1):
            seg = ops[bounds[si]:bounds[si + 1]]
            if si > 0:
                self._patch_fence(new_ops, [o for o in seg if o.name == "fence"])
            queues = {e: [o for o in seg if o.eng == e] for e in self.ENGS}
            heads = {e: 0 for e in self.ENGS}
            done = set(id(o) for o in new_ops)
            nleft = len(seg)
            taken = set()
            while nleft:
                best = None
                for e in self.ENGS:
                    q = queues[e]
                    h = heads[e]
                    while h < len(q) and id(q[h]) in taken:
                        h += 1
                    heads[e] = h
                    cnt = 0
                    i = h
                    emin = None
                    cands = []
                    while i < len(q) and cnt < window:
                        o = q[i]
                        i += 1
                        if id(o) in taken:
                            continue
                        cnt += 1
                        ok = True
                        rdy = 0.0
                        for p in o.preds:
                            if id(p) not in done:
                                ok = False
                                break
                            f = finish[id(p)] + ((0.0 if e == "pe" else 0.12) if p.eng == e and not p.is_dma else xlat)
                            if f > rdy:
                                rdy = f
                        if not ok:
                            if e == "sp" or o.fn is None:
                                break
                            continue
                        st = max(rdy, eng_free[e])
                        cands.append((st, o))
                        if emin is None or st < emin:
                            emin = st
                        if e == "sp":
                            break
                    if not cands:
                        continue
                    pick = min((c for c in cands if c[0] <= emin + slack), key=lambda c: c[1].idx)
                    if best is None or pick[0] < best[0] or (pick[0] == best[0] and pick[1].idx < best[1].idx):
                        best = pick
                assert best is not None, "scheduler stuck"
                st, o = best
                taken.add(id(o))
                done.add(id(o))
                finish[id(o)] = st + o.dur
                if o.is_dma:
                    eng_free[o.eng] = st + 0.1
                else:
                    eng_free[o.eng] = st + o.dur
                new_ops.append(o)
                nleft -= 1
            if os.environ.get("VERB"):
                print("segment", si, "est finish us", max(finish.values()) if finish else 0.0, "ops", len(seg),
                      {e: round(sum(o.dur for o in seg if o.eng == e and not o.is_dma)) for e in self.ENGS})
        self.ops = new_ops
        self.est_total = max(finish.values()) if finish else 0.0

    def resolve(self):
        for op in self.ops:
            for d in op.deps:
                if not d.is_dma:
                    d.signal = True
        cnt = {e: 0 for e in self.ENGS}
        for op in self.ops:
            if op.signal:
                cnt[op.eng] += 1
                op.sig_idx = cnt[op.eng]
        known = {e: {} for e in self.ENGS}
        for op in self.ops:
            need = {}
            for d in op.deps:
                if d.is_dma:
                    key = ("dma", d.dma_sem)
                    val = d.dma_val
                else:
                    key = ("eng", d.eng)
                    val = d.sig_idx
                if need.get(key, 0) < val:
                    need[key] = val
            k = known[op.eng]
            for key, val in need.items():
                if k.get(key, 0) >= val:
                    continue
                k[key] = val
                op.waits.append((key, val))

    def emit(self, nc, es):
        if self.do_reorder:
            self.reorder(segments=self.fences)
        else:
            for fi in self.fences:
                self._patch_fence(self.ops[:fi], [o for o in self.ops[fi:] if o.name == "fence"])
        self.resolve()
        sems = {}
        for e in self.ENGS:
            sems[("eng", e)] = es.enter_context(nc.semaphore("s_" + e))
        for i, k in enumerate(self.dma_counts):
            sems[("dma", k)] = es.enter_context(nc.semaphore("d%d" % i))
        block = es.enter_context(nc.Block())
        per = {e: [o for o in self.ops if o.eng == e] for e in self.ENGS}

        def run(engh, e):
            for op in per[e]:
                for key, val in op.waits:
                    engh.wait_ge(sems[key], val)
                if op.fn is None:
                    continue
                ins = op.fn(engh)
                if op.is_dma:
                    ins.then_inc(sems[("dma", op.dma_sem)], 16)
                elif op.signal:
                    ins.then_inc(sems[("eng", e)], 1)

        @block.tensor
        def _(t):
            run(t, "pe")

        @block.scalar
        def _(t):
            run(t, "act")

        @block.vector
        def _(t):
            run(t, "dve")

        @block.gpsimd
        def _(t):
            run(t, "pool")

        @block.sync
        def _(t):
            run(t, "sp")


def _fsz(ap):
    sh = ap.shape
    n = 1
    for d in sh[1:]:
        n *= int(d)
    return n


def OPC(name, *a, **k):
    fn = lambda e: getattr(e, name)(*a, **k)
    try:
        if name == "matmul":
            fn.n = _fsz(k["rhs"])
        elif name == "transpose":
            fn.n = 128
        elif name == "dma_start":
            fn.n = _fsz(k["out"])
        else:
            fn.n = _fsz(k["out"] if "out" in k else a[0])
    except Exception:
        fn.n = 256
    fn.opname = name
    return fn


class DB:
    def __init__(self, name, views):
        self.name, self.v = name, views

    def __call__(self, T):
        return self.v[T % len(self.v)]

    def k(self, T):
        return (self.name, T % len(self.v))


class Arena:
    def __init__(self, ap, n):
        self.ap = ap
        self.n = n
        self.off = 0
        self.hi = 0

    def alloc(self, shape, dt):
        nel = int(np.prod(shape))
        nw = nel if dt == F32 else (nel + 1) // 2
        nw = (nw + 7) // 8 * 8
        assert self.off + nw <= self.n, "SBUF arena overflow: need %d have %d" % (self.off + nw, self.n)
        v = self.ap[:, self.off:self.off + nw]
        self.off += nw
        self.hi = max(self.hi, self.off)
        if dt != F32:
            v = v.bitcast(dt)
        v = v[:, 0:nel]
        if len(shape) == 2:
            v = v.rearrange("p (a b) -> p a b", a=shape[0])
        elif len(shape) == 3:
            v = v.rearrange("p (a b c) -> p a b c", a=shape[0], b=shape[1])
        return v


def _consts(nt):
    s = nt * 128
    def _tables(dim, theta):
        try:
            import jax
            import jax.numpy as jnp
            with jax.default_device(jax.devices("cpu")[0]):
                pos = jnp.arange(s, dtype=jnp.float32)
                inv = theta ** (-jnp.arange(0, dim, 2, dtype=jnp.float32) / dim)
                ang = pos[:, None] * inv[None, :]
                c, sn = np.asarray(jnp.cos(ang), dtype=np.float32), np.asarray(jnp.sin(ang), dtype=np.float32)
            assert c.shape == (s, dim // 2) and np.isfinite(c).all() and np.isfinite(sn).all()
            return c, sn
        except Exception:
            pos = np.arange(s, dtype=np.float32)
            inv = (np.float32(theta) ** (-np.arange(0, dim, 2, dtype=np.float32) / np.float32(dim))).astype(np.float32)
            ang = (pos[:, None] * inv[None, :]).astype(np.float32)
            return np.cos(ang).astype(np.float32), np.sin(ang).astype(np.float32)

    ca, sa = _tables(16, 500000.0)
    cca = np.concatenate([ca, ca], -1)
    ssa = np.concatenate([-sa, sa], -1)
    cr, sr = _tables(128, 10000.0)
    ropeR = np.concatenate([cr, cr, -sr, sr, cca, ssa], -1).astype(np.float32)
    h = np.arange(RH, dtype=np.float64)
    lg = np.log1p(-np.exp2(-5.0 - h))
    idx = np.arange(128, dtype=np.float64)
    scale = RDK ** -0.5
    mr = np.exp(-(idx[:, None, None] + 1.0) * lg[None, :, None]) * scale * (idx[:, None, None] <= idx[None, None, :])
    kdec = np.exp((127.0 - idx)[:, None] * lg[None, :]) * scale
    eps4 = 4.0 * 1e-6 * np.exp(-2.0 * (idx[:, None] + 1.0) * lg[None, :])
    cdec = np.exp(128.0 * lg)
    k_ = np.arange(128)[:, None]
    q_ = np.arange(128)[None, :]
    maskA = np.stack([(k_ <= q_), (k_ > q_)], 1).astype(np.float32)
    small = np.concatenate([kdec, eps4], -1).astype(np.float32)
    return dict(
        ident=np.eye(128, dtype=np.float32).astype(ml_dtypes.bfloat16),
        ropeR=np.ascontiguousarray(ropeR),
        maskR=np.ascontiguousarray(mr.reshape(128, 512)).astype(np.float32),
        maskA=np.ascontiguousarray(maskA.reshape(128, 256)).astype(ml_dtypes.bfloat16),
        small=small,
    ), [float(c) for c in cdec]


class _Stop(Exception):
    pass


def build_program(nt=32, taps=(), stop=None):
    s = nt * 128
    _, cdec = _consts(1)
    nc = bass.Bass("TRN2", target_bir_lowering=False)

    def din(name, shape, dt=F32):
        return nc.dram_tensor(name, list(shape), dt, kind="ExternalInput").ap()

    x_d = din("x", [s, D])
    win_d = din("w_in", [D, INW])
    wau_d = din("w_att_up", [512, D])
    wru_d = din("w_ret_up", [D, D])
    wo_d = din("w_out", [D, D])
    w1_d = din("w_ff1", [D, DFF])
    w2_d = din("w_ff2", [DFF, D])
    gains_d = din("gains", [128, 24])
    gfin_d = din("g_fin", [1, D])
    bg_d = din("b_gates", [1, 2 * D])
    sinks_d = din("sinks", [1, AH])
    ident_d = din("ident", [128, 128], BF16)
    ropeR_d = din("ropeR", [s, 288])
    maskR_d = din("maskR", [128, 512])
    maskA_d = din("maskA", [128, 256], BF16)
    small_d = din("small", [128, 8])
    out_d = nc.dram_tensor("out", [s, D], F32, kind="ExternalOutput").ap()
    import os
    mT_d = nc.dram_tensor("mT_scratch", [nt, 128, D], BF16, kind=os.environ.get("MTKIND", "Internal")).ap()
    wo_b = nc.dram_tensor("wo_bf16", [D, D], BF16, kind="Internal").ap()
    w1_b = nc.dram_tensor("w1_bf16", [D, DFF], BF16, kind="Internal").ap()
    w2_b = nc.dram_tensor("w2_bf16", [DFF, D], BF16, kind="Internal").ap()
    tap_d = {}
    for name, shape in taps:
        tap_d[name] = nc.dram_tensor("tap_" + name, [nt, 128] + list(shape), F32, kind="ExternalOutput").ap()

    S = Sched()
    NARENA = 53120
    with ExitStack() as es:
        arena_t = es.enter_context(nc.sbuf_tensor("arena", [128, NARENA], F32))
        psum_t = es.enter_context(nc.psum_tensor("psum", [128, 4096], F32))
        AR = Arena(arena_t, NARENA)

        def bank(i):
            return psum_t[:, i * 512:(i + 1) * 512]

        def bank_bf(i):
            return psum_t[:, i * 512:(i + 1) * 512].bitcast(BF16).rearrange("p (c n) -> p c n", c=8)

        bank_rr = [0]
        NF = 6
        tr_rr = [0]

        def next_bank():
            b = bank_rr[0]
            bank_rr[0] = (b + 1) % NF
            return b

        def next_tr():
            b = 6 + tr_rr[0]
            tr_rr[0] = (tr_rr[0] + 1) % 2
            return b

        ident = AR.alloc([128], BF16)
        gains = AR.alloc([24], F32)
        small = AR.alloc([8], F32)
        mhalf = AR.alloc([8], F32)
        rstd = AR.alloc([2], F32)
        ssq = AR.alloc([2], F32)
        S.dma("sp", OPC("dma_start", out=ident, in_=ident_d), "c_id", writes=["ident"])
        S.dma("sp", OPC("dma_start", out=gains, in_=gains_d), "c_g", writes=["gains"])
        S.dma("sp", OPC("dma_start", out=small, in_=small_d), "c_sm", writes=["small"])
        S.add("pool", OPC("memset", mhalf, -0.5), writes=["mhalf"])
        common_off = AR.off

        def gbc(i, n=128):
            return gains[:, i * 8:(i + 1) * 8].unsqueeze(2).to_broadcast([128, 8, n])

        def rmsnorm_T(src, srckey, dst_bf, dstkey, par, gi, xs, xskey):
            sq, rs = ssq[:, par:par + 1], rstd[:, par:par + 1]
            S.add("act", OPC("activation", out=xs, in_=src, func=AF.Square, accum_out=sq),
                  reads=[srckey], writes=[xskey, ("ssq", par)])
            S.add("dve", OPC("tensor_scalar", out=rs, in0=sq, scalar1=1.0 / D, scalar2=EPS, op0=ALU.mult, op1=ALU.add),
                  reads=[("ssq", par)], writes=[("rstd", par)])
            S.add("pool", OPC("tensor_tensor", out=rs, in0=rs, in1=mhalf[:, 0:1], op=ALU.pow),
                  reads=[("rstd", par), "mhalf"], writes=[("rstd", par)])
            S.add("act", OPC("activation", out=xs, in_=src, func=AF.Copy, scale=rs),
                  reads=[srckey, ("rstd", par)], writes=[xskey])
            tb = next_tr()
            for c in range(8):
                S.add("pe", OPC("transpose", out=bank_bf(tb)[:, c, :], in_=xs[:, c * 128:(c + 1) * 128], identity=ident),
                      reads=[xskey, "ident"], writes=[("ps", tb)])
            S.add("dve", OPC("tensor_tensor", out=dst_bf, in0=bank_bf(tb), in1=gbc(gi), op=ALU.mult),
                  reads=[("ps", tb), "gains"], writes=[dstkey])

        def tap(name, T, src, key):
            if name in tap_d:
                S.dma("sp", OPC("dma_start", out=tap_d[name][T], in_=src), ("tap", name), reads=[key])

        Win = AR.alloc([8, INW], BF16)
        Wau = AR.alloc([4, D], BF16)
        Wru = AR.alloc([8, D], BF16)
        maskR = AR.alloc([4, 128], F32)
        maskA = AR.alloc([2, 128], BF16)
        esink = AR.alloc([8], F32)
        bhl = AR.alloc([2 * D], BF16)
        y_sb = AR.alloc([D], F32)
        ones1 = AR.alloc([128], BF16)
        xt = DB("xt", [AR.alloc([D], F32) for _ in range(2)])
        rR = DB("rR", [AR.alloc([288], F32) for _ in range(2)])
        xs = AR.alloc([D], BF16)
        xnT = DB("xnT", [AR.alloc([8, 128], BF16) for _ in range(2)])
        qa_sb = AR.alloc([512], BF16)
        ka_sb = AR.alloc([128], BF16)
        va_aug = DB("va", [AR.alloc([2, 66], BF16) for _ in range(2)])
        rtmpA = AR.alloc([10, 16], F32)
        rtmpB = AR.alloc([10, 16], F32)
        rt12 = AR.alloc([D], F32)
        rt1, rt2 = rt12[:, 0:512], rt12[:, 512:1024]
        qr_sb = AR.alloc([512], BF16)
        kr_sb = AR.alloc([512], BF16)
        kd_sb = DB("kd", [AR.alloc([4, 128], BF16) for _ in range(2)])
        vr_sb = DB("vr", [AR.alloc([D], BF16) for _ in range(2)])
        tg = AR.alloc([512], F32)
        sg = DB("sg", [AR.alloc([D], F32) for _ in range(2)])
        qaT = DB("qaT", [AR.alloc([4, 128], BF16) for _ in range(2)])
        kaT = DB("kaT", [AR.alloc([128], BF16) for _ in range(2)])
        qkT = DB("qkT", [AR.alloc([8, 128], BF16) for _ in range(2)])
        p_sb = AR.alloc([4, 512], BF16)
        den = AR.alloc([8], F32)
        att_sb = AR.alloc([512], BF16)
        attT = AR.alloc([4, 128], BF16)
        am_sb = AR.alloc([4, 128], BF16)
        S32 = AR.alloc([4, 256], F32)
        Sbf = DB("Sbf", [AR.alloc([4, 256], BF16) for _ in range(2)])
        gst = AR.alloc([4, 6], F32)
        gmv = AR.alloc([4, 2], F32)
        grs = AR.alloc([4], F32)
        gated = AR.alloc([D], BF16)
        gatedT = AR.alloc([8, 128], BF16)
        tgab = [(AR.alloc([512], F32), AR.alloc([512], F32)) for _ in range(2)]
        merged = AR.alloc([D], BF16)
        mergedT = DB("mT", [AR.alloc([8, 128], BF16) for _ in range(1)])
        phaseA_hi = AR.off
        if os.environ.get('VERB'):
            print('phaseA words', phaseA_hi, 'of', NARENA)

        S.dma("sp", OPC("dma_start", out=maskR, in_=maskR_d.rearrange("p (h q) -> p h q", h=4)), "c_mr", writes=["maskR"])
        S.dma("sp", OPC("dma_start", out=maskA, in_=maskA_d.rearrange("p (b q) -> p b q", b=2)), "c_ma", writes=["maskA"])
        S.dma("sp", OPC("dma_start", out=esink, in_=sinks_d.partition_broadcast(128)), "c_sk", writes=["esink"])
        S.add("act", OPC("activation", out=esink, in_=esink, func=AF.Exp), reads=["esink"], writes=["esink"])
        S.add("pool", OPC("memset", ones1, 1.0), writes=["ones1"])
        for p_ in range(2):
            S.add("pool", OPC("memset", va_aug(p_), 1.0), writes=[va_aug.k(p_)])
        S.add("dve", OPC("memset", bhl[0:64, :], 0.0), writes=["bhl"])
        for hb in range(2):
            cs_ = slice(hb * D, (hb + 1) * D)
            S.dma("sp", OPC("dma_start", out=rt12[0:1, :], in_=bg_d[:, cs_]), "c_bg", writes=["rt1", "rt2"])
            S.dma("sp", OPC("dma_start", out=rt12[32:33, :], in_=bg_d[:, cs_]), "c_bg2", writes=["rt1", "rt2"])
            S.add("dve", OPC("tensor_copy", out=bhl[0:1, cs_], in_=rt12[0:1, :]), reads=["rt1", "rt2"], writes=["bhl"])
            S.add("dve", OPC("tensor_copy", out=xs[32:33, :], in_=rt12[32:33, :]), reads=["rt1", "rt2"], writes=["xs"])
            S.add("dve", OPC("tensor_tensor", out=bhl[32:33, cs_], in0=rt12[32:33, :], in1=xs[32:33, :], op=ALU.subtract),
                  reads=["rt1", "rt2", "xs"], writes=["bhl"])

        def wload(dst, src, key):
            sk = os.environ.get("SKIPW", "")
            if sk == "1" or key in sk.split(",") or (sk.startswith("only:") and key not in sk[5:].split(",")):
                return
            S.dma("pool", OPC("dma_start", out=dst, in_=src), ("w", key), writes=[("w", key)])

        win_v = win_d.rearrange("(c p) n -> p c n", p=128)
        for j in range(4):
            for a in range(2):
                hh = a * 4 + j
                wload(Win[:, :, j * 128 + a * 64: j * 128 + a * 64 + 64], win_v[:, :, hh * 64:(hh + 1) * 64], "qa")
        groups = [("kava", OFF_KA, 256), ("qr", OFF_QR, 512), ("kr", OFF_KR, 512), ("vr0", OFF_VR, 512),
                  ("vr1", OFF_VR + 512, 512), ("gr0", OFF_GR, 512), ("gr1", OFF_GR + 512, 512)]
        for key, c0, ncol in groups:
            wload(Win[:, :, c0:c0 + ncol], win_v[:, :, c0:c0 + ncol], key)
        for key, c0 in (("ga0", OFF_GA), ("ga1", OFF_GA + 512), ("gt0", OFF_GRT), ("gt1", OFF_GRT + 512)):
            wload(Win[:, :, c0:c0 + 512], win_v[:, :, c0:c0 + 512], key)
        wload(Wau, wau_d.rearrange("(c p) n -> p c n", p=128), "wau")
        wload(Wru[:, 0:4, :], wru_d.rearrange("(c p) n -> p c n", p=128)[:, 0:4, :], "wru0")
        wload(Wru[:, 4:8, :], wru_d.rearrange("(c p) n -> p c n", p=128)[:, 4:8, :], "wru1")

        wo_v = wo_d.rearrange("(c p) n -> p c n", p=128)
        w1_v = w1_d.rearrange("(c p) n -> p c n", p=128)
        w2_v = w2_d.rearrange("(c p) n -> p c n", p=128)
        wo_bv = wo_b.rearrange("(c p) n -> p c n", p=128)
        w1_bv = w1_b.rearrange("(c p) n -> p c n", p=128)
        w2_bv = w2_b.rearrange("(c p) n -> p c n", p=128)
        def precast_B():
            for n in range(2):
                S.dma("pool", OPC("dma_start", out=wo_bv[:, :, n * 512:(n + 1) * 512], in_=wo_v[:, :, n * 512:(n + 1) * 512]),
                      ("cb", "wo%d" % n), writes=[("wbs", "wo%d" % n)])
            for j in range(8):
                S.dma("pool", OPC("dma_start", out=w1_bv[:, :, j * 512:(j + 1) * 512], in_=w1_v[:, :, j * 512:(j + 1) * 512]),
                      ("cb", "w1_%d" % j), writes=[("wbs", "w1_%d" % j)])
            for j in range(8):
                S.dma("pool", OPC("dma_start", out=w2_bv[:, j * 4:(j + 1) * 4, :], in_=w2_v[:, j * 4:(j + 1) * 4, :]),
                      ("cb", "w2_%d" % j), writes=[("wbs", "w2_%d" % j)])

        def load_A(T):
            par = T % 2
            S.dma("sp", OPC("dma_start", out=xt(T), in_=x_d[T * 128:(T + 1) * 128, :]), ("x", par), writes=[xt.k(T)])
            S.dma("sp", OPC("dma_start", out=rR(T), in_=ropeR_d[T * 128:(T + 1) * 128, :]), rR.k(T), writes=[rR.k(T)])

        def proj(T, par, c0, ncol, wkeys, bias_c0=None):
            b = next_bank()
            for c in range(8):
                S.add("pe", OPC("matmul", bank(b)[:, 0:ncol], lhsT=xnT(T)[:, c, :], rhs=Win[:, c, c0:c0 + ncol],
                                                    start=(c == 0), stop=(c == 7 and bias_c0 is None)),
                      reads=[xnT.k(T)] + [("w", k) for k in wkeys], writes=[("ps", b)])
            if bias_c0 is not None:
                S.add("pe", OPC("matmul", bank(b)[:, 0:ncol], lhsT=ones1[0:33, :], rhs=bhl[0:33, bias_c0:bias_c0 + ncol], start=False, stop=True),
                      reads=["ones1", "bhl"], writes=[("ps", b)])
            return b

        def rope_small(psv, nh, dst, dstkey, T, b):
            cc = rR(T)[:, 256:272].unsqueeze(1).to_broadcast([128, nh, 16])
            sn = rR(T)[:, 272:280].unsqueeze(1).to_broadcast([128, nh, 8])
            sp_ = rR(T)[:, 280:288].unsqueeze(1).to_broadcast([128, nh, 8])
            ta_, tb_ = rtmpA[:, 0:nh, :], rtmpB[:, 0:nh, :]
            S.add("dve", OPC("tensor_tensor", out=ta_, in0=psv[:, :, 0:16], in1=cc, op=ALU.mult),
                  reads=[("ps", b), rR.k(T)], writes=["rtmpA"])
            S.add("dve", OPC("tensor_tensor", out=tb_[:, :, 0:8], in0=psv[:, :, 8:16], in1=sn, op=ALU.mult),
                  reads=[("ps", b), rR.k(T)], writes=["rtmpB"])
            S.add("dve", OPC("tensor_tensor", out=tb_[:, :, 8:16], in0=psv[:, :, 0:8], in1=sp_, op=ALU.mult),
                  reads=[("ps", b), rR.k(T)], writes=["rtmpB"])
            S.add("pool", OPC("tensor_tensor", out=dst[:, :, 0:16], in0=ta_, in1=tb_, op=ALU.add),
                  reads=["rtmpA", "rtmpB"], writes=[dstkey])

        def rope_big(b, T):
            cc = rR(T)[:, 0:128].unsqueeze(1).to_broadcast([128, 4, 128])
            sn = rR(T)[:, 128:192].unsqueeze(1).to_broadcast([128, 4, 64])
            sp_ = rR(T)[:, 192:256].unsqueeze(1).to_broadcast([128, 4, 64])
            r1 = rt1.rearrange("p (h d) -> p h d", h=4)
            r2 = rt2.rearrange("p (h d) -> p h d", h=4)
            S.add("act", OPC("activation", out=rt1, in_=bank(b), func=AF.Copy), reads=[("ps", b)], writes=["rt1"])
            S.add("dve", OPC("tensor_tensor", out=r2[:, :, 0:64], in0=r1[:, :, 64:128], in1=sn, op=ALU.mult),
                  reads=["rt1", rR.k(T)], writes=["rt2"])
            S.add("dve", OPC("tensor_tensor", out=r2[:, :, 64:128], in0=r1[:, :, 0:64], in1=sp_, op=ALU.mult),
                  reads=["rt1", rR.k(T)], writes=["rt2"])
            S.add("dve", OPC("tensor_tensor", out=r1, in0=r1, in1=cc, op=ALU.mult),
                  reads=["rt1", rR.k(T)], writes=["rt1"])

        def stage1(T):
            par = T % 2
            rmsnorm_T(xt(T), xt.k(T), xnT(T), xnT.k(T), par, 0, xs, "xs")
            yield
            b = proj(T, par, OFF_QA, 512, ["qa"])
            S.add("act", OPC("activation", out=qa_sb, in_=bank(b), func=AF.Copy), reads=[("ps", b)], writes=["qa_sb"])
            rope_small(bank(b).rearrange("p (h d) -> p h d", h=8), 8, qa_sb.rearrange("p (h d) -> p h d", h=8), "qa_sb", T, b)
            yield
            b = proj(T, par, OFF_KA, 256, ["kava"])
            S.add("act", OPC("activation", out=ka_sb, in_=bank(b)[:, 0:128], func=AF.Copy), reads=[("ps", b)], writes=["ka_sb"])
            S.add("act", OPC("activation", out=va_aug(T)[:, :, 0:64], in_=bank(b)[:, 128:256].rearrange("p (g d) -> p g d", g=2),
                                                func=AF.Copy), reads=[("ps", b)], writes=[va_aug.k(T)])
            rope_small(bank(b)[:, 0:128].rearrange("p (h d) -> p h d", h=2), 2, ka_sb.rearrange("p (h d) -> p h d", h=2), "ka_sb", T, b)
            yield
            b = proj(T, par, OFF_QR, 512, ["qr"])
            rope_big(b, T)
            S.add("pool", OPC("tensor_tensor", out=qr_sb, in0=rt1, in1=rt2, op=ALU.add), reads=["rt1", "rt2"], writes=["qr_sb"])
            yield
            b = proj(T, par, OFF_KR, 512, ["kr"])
            rope_big(b, T)
            S.add("pool", OPC("tensor_tensor", out=rt1, in0=rt1, in1=rt2, op=ALU.add), reads=["rt1", "rt2"], writes=["rt1"])
            S.add("act", OPC("activation", out=kr_sb, in_=rt1, func=AF.Copy), reads=["rt1"], writes=["kr_sb"])
            S.add("pool", OPC("tensor_tensor", out=kd_sb(T), in0=rt1.rearrange("p (h d) -> p h d", h=4),
                                                    in1=small[:, 0:4].unsqueeze(2).to_broadcast([128, 4, 128]), op=ALU.mult),
                  reads=["rt1", "small"], writes=[kd_sb.k(T)])
            yield
            for i in range(2):
                b = proj(T, par, OFF_VR + i * 512, 512, ["vr%d" % i])
                S.add("act", OPC("activation", out=vr_sb(T)[:, i * 512:(i + 1) * 512], in_=bank(b), func=AF.Copy),
                      reads=[("ps", b)], writes=[vr_sb.k(T)])
            yield
            for i in range(2):
                b = proj(T, par, OFF_GR + i * 512, 512, ["gr%d" % i])
                S.add("act", OPC("activation", out=tg, in_=bank(b), func=AF.Tanh, scale=0.5), reads=[("ps", b)], writes=["tg"])
                S.add("dve", OPC("scalar_tensor_tensor", out=sg(T)[:, i * 512:(i + 1) * 512], in0=tg, scalar=1.0, in1=bank(b),
                                                                        op0=ALU.add, op1=ALU.mult),
                      reads=["tg", ("ps", b)], writes=[sg.k(T)])
            yield
            tb = next_tr()
            for j in range(4):
                S.add("pe", OPC("transpose", out=bank_bf(tb)[:, j, :], in_=qa_sb[:, j * 128:(j + 1) * 128], identity=ident),
                      reads=["qa_sb", "ident"], writes=[("ps", tb)])
            S.add("pe", OPC("transpose", out=bank_bf(tb)[:, 4, :], in_=ka_sb, identity=ident),
                  reads=["ka_sb", "ident"], writes=[("ps", tb)])
            S.add("act", OPC("activation", out=qaT(T), in_=bank_bf(tb)[:, 0:4, :], func=AF.Copy), reads=[("ps", tb)], writes=[qaT.k(T)])
            S.add("dve", OPC("tensor_copy", out=kaT(T), in_=bank_bf(tb)[:, 4, :]), reads=[("ps", tb)], writes=[kaT.k(T)])
            tb2 = next_tr()
            for hh in range(4):
                S.add("pe", OPC("transpose", out=bank_bf(tb2)[:, hh, :], in_=qr_sb[:, hh * 128:(hh + 1) * 128], identity=ident),
                      reads=["qr_sb", "ident"], writes=[("ps", tb2)])
            for hh in range(4):
                S.add("pe", OPC("transpose", out=bank_bf(tb2)[:, 4 + hh, :], in_=kr_sb[:, hh * 128:(hh + 1) * 128], identity=ident),
                      reads=["kr_sb", "ident"], writes=[("ps", tb2)])
            S.add("act", OPC("activation", out=qkT(T), in_=bank_bf(tb2), func=AF.Copy), reads=[("ps", tb2)], writes=[qkT.k(T)])

        def stage2(T, st_ops):
            par = T % 2
            blks = [(0, T)] + ([(1, T - 1)] if T > 0 else [])
            for g in range(2):
                for (bi, bp) in blks:
                    b = next_bank()
                    idx = g * 2 + bi
                    S.add("pe", OPC("matmul", bank(b), lhsT=kaT(bp)[g * 64:(g + 1) * 64, :], rhs=qaT(T)[g * 64:(g + 1) * 64, :, :],
                                    start=True, stop=True), reads=[kaT.k(bp), qaT.k(T)], writes=[("ps", b)])
                    S.add("act", OPC("activation", out=p_sb[:, idx, :], in_=bank(b), func=AF.Exp, scale=ADH ** -0.5),
                          reads=[("ps", b)], writes=[("p", idx)])
                    pv4 = p_sb[:, idx, :].rearrange("p (j q) -> p j q", j=4)
                    S.add("pool", OPC("tensor_tensor", out=pv4, in0=pv4, in1=maskA[:, bi:bi + 1, :].to_broadcast([128, 4, 128]), op=ALU.mult),
                          reads=[("p", idx), "maskA"], writes=[("p", idx)])
            b = next_bank()
            for h in range(4):
                S.add("pe", OPC("matmul", bank(b)[:, h * 128:(h + 1) * 128], lhsT=qkT(T)[:, 4 + h, :], rhs=qkT(T)[:, h, :], start=True, stop=True),
                      reads=[qkT.k(T)], writes=[("ps", b)])
            S.add("dve", OPC("tensor_tensor", out=am_sb, in0=bank(b).rearrange("p (h q) -> p h q", h=4), in1=maskR, op=ALU.mult),
                  reads=[("ps", b), "maskR"], writes=["am_sb"])
            yield
            pvb = [next_bank(), next_bank()]
            for h in range(8):
                g, j = h // 4, h % 4
                for n_, (bi, bp) in enumerate(blks):
                    idx = g * 2 + bi
                    S.add("pe", OPC("matmul", bank(pvb[g])[:, j * 65:(j + 1) * 65], lhsT=p_sb[:, idx, j * 128:(j + 1) * 128],
                                    rhs=va_aug(bp)[:, g, 0:65], start=(n_ == 0), stop=(n_ == len(blks) - 1)),
                          reads=[("p", idx), va_aug.k(bp)], writes=[("ps", pvb[g])])
            yb = [next_bank(), next_bank()]
            for h in range(4):
                reg = bank(yb[h // 2])[:, (h % 2) * 256:(h % 2 + 1) * 256]
                S.add("pe", OPC("matmul", reg, lhsT=am_sb[:, h, :], rhs=vr_sb(T)[:, h * 256:(h + 1) * 256], start=True, stop=(T == 0)),
                      reads=["am_sb", vr_sb.k(T)], writes=[("ps", yb[h // 2])])
                if T > 0:
                    S.add("pe", OPC("matmul", reg, lhsT=qkT(T)[:, h, :], rhs=Sbf(T - 1)[:, h, :], start=False, stop=True),
                          reads=[qkT.k(T), Sbf.k(T - 1)], writes=[("ps", yb[h // 2])])
            for g in range(2):
                pv = bank(pvb[g])[:, 0:260].rearrange("p (j d) -> p j d", j=4)
                dn = den[:, g * 4:(g + 1) * 4]
                S.add("dve", OPC("tensor_tensor", out=dn.unsqueeze(2), in0=pv[:, :, 64:65], in1=esink[:, g * 4:(g + 1) * 4].unsqueeze(2), op=ALU.add),
                      reads=[("ps", pvb[g]), "esink"], writes=[("den", g)])
                S.add("dve", OPC("reciprocal", out=dn, in_=dn), reads=[("den", g)], writes=[("den", g)])
                S.add("dve", OPC("tensor_tensor", out=att_sb[:, g * 256:(g + 1) * 256].rearrange("p (j d) -> p j d", j=4), in0=pv[:, :, 0:64],
                                 in1=dn.unsqueeze(2).to_broadcast([128, 4, 64]), op=ALU.mult),
                      reads=[("ps", pvb[g]), ("den", g)], writes=[("att_sb", g)])
            yield
            for h in range(4):
                reg = bank(yb[h // 2])[:, (h % 2) * 256:(h % 2 + 1) * 256]
                S.add("dve", OPC("bn_stats", out=gst[:, h, :], in_=reg), reads=[("ps", yb[h // 2])], writes=[("gst", h)])
                if h % 2 == 1:
                    i_ = h // 2
                    S.add("dve", OPC("tensor_copy", out=y_sb[:, i_ * 512:(i_ + 1) * 512], in_=bank(yb[i_])),
                          reads=[("ps", yb[i_])], writes=[("y_sb", i_)])
            for h in range(4):
                S.add("dve", OPC("bn_aggr", out=gmv[:, h, :], in_=gst[:, h, :]), reads=[("gst", h)], writes=[("gmv", h)])
            S.add("dve", OPC("scalar_tensor_tensor", out=grs.unsqueeze(2), in0=gmv[:, :, 1:2], scalar=4.0, in1=small[:, 4:8].unsqueeze(2),
                             op0=ALU.mult, op1=ALU.add), reads=[("gmv", 0), ("gmv", 1), ("gmv", 2), ("gmv", 3), "small"], writes=["grs"])
            S.add("pool", OPC("tensor_tensor", out=grs, in0=grs, in1=mhalf[:, 0:4], op=ALU.pow), reads=["grs", "mhalf"], writes=["grs"])
            S.add("pool", OPC("tensor_tensor", out=sg(T).rearrange("p (h d) -> p h d", h=4), in0=sg(T).rearrange("p (h d) -> p h d", h=4),
                              in1=grs.unsqueeze(2).to_broadcast([128, 4, 256]), op=ALU.mult), reads=[sg.k(T), "grs"], writes=[sg.k(T)])
            if T < nt - 1:
                kb = [next_bank(), next_bank()]
                for h in range(4):
                    reg = bank(kb[h // 2])[:, (h % 2) * 256:(h % 2 + 1) * 256]
                    S.add("pe", OPC("matmul", reg, lhsT=kd_sb(T)[:, h, :], rhs=vr_sb(T)[:, h * 256:(h + 1) * 256], start=True, stop=True),
                          reads=[kd_sb.k(T), vr_sb.k(T)], writes=[("ps", kb[h // 2])])
            tb = next_tr()
            for c in range(4):
                S.add("pe", OPC("transpose", out=bank_bf(tb)[:, c, :], in_=att_sb[:, c * 128:(c + 1) * 128], identity=ident),
                      reads=[("att_sb", c // 2), "ident"], writes=[("ps", tb)])
            S.add("act", OPC("activation", out=attT, in_=bank_bf(tb)[:, 0:4, :], func=AF.Copy), reads=[("ps", tb)], writes=["attT"])
            for h in range(4):
                S.add("dve", OPC("scalar_tensor_tensor", out=gated[:, h * 256:(h + 1) * 256], in0=y_sb[:, h * 256:(h + 1) * 256], scalar=gmv[:, h, 0:1],
                                 in1=sg(T)[:, h * 256:(h + 1) * 256], op0=ALU.subtract, op1=ALU.mult),
                      reads=[("y_sb", h // 2), ("gmv", h), sg.k(T)], writes=[("gated", h)])
            if T < nt - 1:
                for h in range(4):
                    reg = bank(kb[h // 2])[:, (h % 2) * 256:(h % 2 + 1) * 256]
                    if T == 0:
                        S.add("act", OPC("activation", out=S32[:, h, :], in_=reg, func=AF.Copy), reads=[("ps", kb[h // 2])], writes=[("S32", h)])
                    else:
                        S.add("dve", OPC("scalar_tensor_tensor", out=S32[:, h, :], in0=S32[:, h, :], scalar=cdec[h], in1=reg,
                                         op0=ALU.mult, op1=ALU.add), reads=[("ps", kb[h // 2]), ("S32", h)], writes=[("S32", h)])
                S.add("act", OPC("activation", out=Sbf(T), in_=S32, func=AF.Copy), reads=[("S32", 0), ("S32", 1), ("S32", 2), ("S32", 3)], writes=[Sbf.k(T)])
            yield
            for n in range(2):
                bga = proj(T, par, OFF_GA + n * 512, 512, ["ga%d" % n], bias_c0=n * 512)
                bgr = proj(T, par, OFF_GRT + n * 512, 512, ["gt%d" % n], bias_c0=D + n * 512)
                tga, tgr = tgab[n]
                S.add("act", OPC("activation", out=tga, in_=bank(bga), func=AF.Tanh, scale=0.5), reads=[("ps", bga)], writes=[("tga", n)])
                S.add("act", OPC("activation", out=tgr, in_=bank(bgr), func=AF.Tanh, scale=0.5), reads=[("ps", bgr)], writes=[("tgr", n)])
                if n == 0:
                    tb = next_tr()
                    for c in range(8):
                        S.add("pe", OPC("transpose", out=bank_bf(tb)[:, c, :], in_=gated[:, c * 128:(c + 1) * 128], identity=ident),
                              reads=[("gated", c // 2), "ident"], writes=[("ps", tb)])
                    S.add("dve", OPC("tensor_tensor", out=gatedT, in0=bank_bf(tb), in1=gbc(1), op=ALU.mult),
                          reads=[("ps", tb), "gains"], writes=["gatedT"])
                    yield
            for n in range(2):
                tga, tgr = tgab[n]
                ba = next_bank()
                for c in range(4):
                    S.add("pe", OPC("matmul", bank(ba), lhsT=attT[:, c, :], rhs=Wau[:, c, n * 512:(n + 1) * 512], start=(c == 0), stop=(c == 3)),
                          reads=["attT", ("w", "wau")], writes=[("ps", ba)])
                br = next_bank()
                for c in range(8):
                    S.add("pe", OPC("matmul", bank(br), lhsT=gatedT[:, c, :], rhs=Wru[:, c, n * 512:(n + 1) * 512], start=(c == 0), stop=(c == 7)),
                          reads=["gatedT", ("w", "wru0"), ("w", "wru1")], writes=[("ps", br)])
                S.add("dve", OPC("scalar_tensor_tensor", out=tga, in0=tga, scalar=1.0, in1=bank(ba), op0=ALU.add, op1=ALU.mult),
                      reads=[("tga", n), ("ps", ba)], writes=[("tga", n)])
                S.add("dve", OPC("scalar_tensor_tensor", out=tgr, in0=tgr, scalar=1.0, in1=bank(br), op0=ALU.add, op1=ALU.mult),
                      reads=[("tgr", n), ("ps", br)], writes=[("tgr", n)])
                S.add("pool", OPC("tensor_tensor", out=merged[:, n * 512:(n + 1) * 512], in0=tga, in1=tgr, op=ALU.add),
                      reads=[("tga", n), ("tgr", n)], writes=[("merged", n)])
            yield
            tb = next_tr()
            for c in range(8):
                S.add("pe", OPC("transpose", out=bank_bf(tb)[:, c, :], in_=merged[:, c * 128:(c + 1) * 128], identity=ident),
                      reads=[("merged", c // 4), "ident"], writes=[("ps", tb)])
            S.add("act", OPC("activation", out=mergedT(T), in_=bank_bf(tb), func=AF.Copy), reads=[("ps", tb)], writes=[mergedT.k(T)])
            st_ops.append(S.dma("sp", OPC("dma_start", out=mT_d[T], in_=mergedT(T).rearrange("p c n -> p (c n)")), ("mTst", par),
                                reads=[mergedT.k(T)]))

        def phaseB(st_ops):
            S.fence_all(extra=st_ops[-2:])
            AR.off = common_off
            Wo = AR.alloc([8, D], BF16)
            W1 = AR.alloc([8, DFF], BF16)
            W2 = AR.alloc([32, D], BF16)
            gfin = AR.alloc([D], F32)
            xb = DB("xb", [AR.alloc([D], F32) for _ in range(2)])
            mTb = DB("mTb", [AR.alloc([8, 128], BF16) for _ in range(2)])
            h1 = DB("h1", [AR.alloc([D], F32) for _ in range(2)])
            junk = AR.alloc([D], BF16)
            h1s = AR.alloc([D], BF16)
            h1nT = AR.alloc([8, 128], BF16)
            rr = [AR.alloc([512], F32) for _ in range(2)]
            hT = AR.alloc([32, 128], BF16)
            hsq = [AR.alloc([512], BF16) for _ in range(2)]
            ob = DB("ob", [AR.alloc([D], F32) for _ in range(2)])

            S.dma("sp", OPC("dma_start", out=gfin, in_=gfin_d.partition_broadcast(128)), "c_gf", writes=["gfin"])
            def wloadB(part):
                for n in (range(2) if part == 0 else ()):
                    S.dma("sp", OPC("dma_start", out=Wo[:, :, n * 512:(n + 1) * 512], in_=wo_bv[:, :, n * 512:(n + 1) * 512]),
                          ("wb", "wo%d" % n), reads=[("wbs", "wo%d" % n)], writes=[("w", "wo%d" % n)], throttle="wB")
                for j in (range(8) if part == 1 else ()):
                    S.dma("sp", OPC("dma_start", out=W1[:, :, j * 512:(j + 1) * 512], in_=w1_bv[:, :, j * 512:(j + 1) * 512]),
                          ("wb", "w1_%d" % j), reads=[("wbs", "w1_%d" % j)], writes=[("w", "w1_%d" % j)], throttle="wB")
                for j in (range(8) if part == 1 else ()):
                    S.dma("sp", OPC("dma_start", out=W2[:, j * 4:(j + 1) * 4, :], in_=w2_bv[:, j * 4:(j + 1) * 4, :]),
                          ("wb", "w2_%d" % j), reads=[("wbs", "w2_%d" % j)], writes=[("w", "w2_%d" % j)], throttle="wB")

            def load_B(T):
                par = T % 2
                S.dma("sp", OPC("dma_start", out=xb(T), in_=x_d[T * 128:(T + 1) * 128, :]), xb.k(T), writes=[xb.k(T)])
                S.dma("sp", OPC("dma_start", out=mTb(T), in_=mT_d[T].rearrange("p (c n) -> p c n", c=8)), mTb.k(T),
                      writes=[mTb.k(T)])

            def p1(T):
                par = T % 2
                for n in range(2):
                    b = next_bank()
                    for c in range(8):
                        S.add("pe", OPC("matmul", bank(b), lhsT=mTb(T)[:, c, :], rhs=Wo[:, c, n * 512:(n + 1) * 512], start=(c == 0), stop=(c == 7)),
                              reads=[mTb.k(T), ("w", "wo%d" % n)], writes=[("ps", b)])
                    S.add("dve", OPC("scalar_tensor_tensor", out=h1(T)[:, n * 512:(n + 1) * 512], in0=bank(b), scalar=0.5,
                                     in1=xb(T)[:, n * 512:(n + 1) * 512], op0=ALU.mult, op1=ALU.add),
                          reads=[("ps", b), xb.k(T)], writes=[h1.k(T)])
                sq, rs = ssq[:, 0:1], rstd[:, 0:1]
                S.add("act", OPC("activation", out=h1s, in_=h1(T), func=AF.Square, accum_out=sq), reads=[h1.k(T)], writes=["h1s", ("ssq", 0)])
                S.add("dve", OPC("tensor_scalar", out=rs, in0=sq, scalar1=1.0 / D, scalar2=EPS, op0=ALU.mult, op1=ALU.add),
                      reads=[("ssq", 0)], writes=[("rstd", 0)])
                S.add("pool", OPC("tensor_tensor", out=rs, in0=rs, in1=mhalf[:, 0:1], op=ALU.pow), reads=[("rstd", 0), "mhalf"], writes=[("rstd", 0)])
                S.add("act", OPC("activation", out=h1s, in_=h1(T), func=AF.Copy, scale=rs), reads=[h1.k(T), ("rstd", 0)], writes=["h1s"])

            def p2(T):
                tb = next_tr()
                for c in range(8):
                    S.add("pe", OPC("transpose", out=bank_bf(tb)[:, c, :], in_=h1s[:, c * 128:(c + 1) * 128], identity=ident),
                          reads=["h1s", "ident"], writes=[("ps", tb)])
                S.add("dve", OPC("tensor_tensor", out=h1nT, in0=bank_bf(tb), in1=gbc(2), op=ALU.mult),
                      reads=[("ps", tb), "gains"], writes=["h1nT"])

            def ff1(T):
                for jb in range(8):
                    b = next_bank()
                    for c in range(8):
                        S.add("pe", OPC("matmul", bank(b), lhsT=h1nT[:, c, :], rhs=W1[:, c, jb * 512:(jb + 1) * 512], start=(c == 0), stop=(c == 7)),
                              reads=["h1nT", ("w", "w1_%d" % jb)], writes=[("ps", b)])
                    r_ = rr[jb % 2]
                    hq = hsq[jb % 2]
                    S.add("act", OPC("activation", out=r_, in_=bank(b), func=AF.Relu), reads=[("ps", b)], writes=[("rr", jb % 2)])
                    S.add("dve", OPC("tensor_tensor", out=hq, in0=r_, in1=r_, op=ALU.mult), reads=[("rr", jb % 2)], writes=[("hsq", jb % 2)])
                    tb = next_tr()
                    for jj in range(4):
                        S.add("pe", OPC("transpose", out=bank_bf(tb)[:, jj, :], in_=hq[:, jj * 128:(jj + 1) * 128], identity=ident),
                              reads=[("hsq", jb % 2), "ident"], writes=[("ps", tb)])
                    S.add("act", OPC("activation", out=hT[:, jb * 4:(jb + 1) * 4, :], in_=bank_bf(tb)[:, 0:4, :], func=AF.Copy),
                          reads=[("ps", tb)], writes=[("hT", jb)])

            def ff2(T):
                par = T % 2
                o_ = ob(T)
                for n in range(2):
                    b = next_bank()
                    for k in range(32):
                        S.add("pe", OPC("matmul", bank(b), lhsT=hT[:, k, :], rhs=W2[:, k, n * 512:(n + 1) * 512], start=(k == 0), stop=(k == 31)),
                              reads=[("hT", k // 4), ("w", "w2_%d" % (k // 4))], writes=[("ps", b)])
                    S.add("dve", OPC("tensor_tensor", out=o_[:, n * 512:(n + 1) * 512], in0=bank(b), in1=h1(T)[:, n * 512:(n + 1) * 512], op=ALU.add),
                          reads=[("ps", b), h1.k(T)], writes=[ob.k(T)])
                sq, rs = ssq[:, 1:2], rstd[:, 1:2]
                S.add("act", OPC("activation", out=junk, in_=o_, func=AF.Square, accum_out=sq), reads=[ob.k(T)], writes=["junk", ("ssq", 1)])
                S.add("dve", OPC("tensor_scalar", out=rs, in0=sq, scalar1=1.0 / D, scalar2=EPS, op0=ALU.mult, op1=ALU.add),
                      reads=[("ssq", 1)], writes=[("rstd", 1)])
                S.add("pool", OPC("tensor_tensor", out=rs, in0=rs, in1=mhalf[:, 0:1], op=ALU.pow), reads=[("rstd", 1), "mhalf"], writes=[("rstd", 1)])
                S.add("dve", OPC("scalar_tensor_tensor", out=o_, in0=o_, scalar=rs, in1=gfin, op0=ALU.mult, op1=ALU.mult),
                      reads=[ob.k(T), ("rstd", 1), "gfin"], writes=[ob.k(T)])
                return S.dma("sp", OPC("dma_start", out=out_d[T * 128:(T + 1) * 128, :], in_=o_), ("ost", par), reads=[ob.k(T)])

            load_B(0)
            wloadB(0)
            if nt > 1:
                load_B(1)
            wloadB(1)
            p1(0)
            p2(0)
            for T in range(nt):
                if T + 1 < nt:
                    p1(T + 1)
                ff1(T)
                if T + 1 < nt:
                    p2(T + 1)
                ff2(T)
                if T + 2 < nt:
                    load_B(T + 2)

        def chk(tag):
            if stop == tag:
                raise _Stop()

        try:
            chk("consts")
            load_A(0)
            if nt > 1:
                load_A(1)
            st_ops = []
            g2 = None
            for T in range(nt + 1):
                g1 = stage1(T) if T < nt else None
                while g1 is not None or g2 is not None:
                    if g1 is not None:
                        try:
                            next(g1)
                        except StopIteration:
                            g1 = None
                    if g2 is not None:
                        try:
                            next(g2)
                        except StopIteration:
                            g2 = None
                if T < nt:
                    if T + 2 < nt:
                        load_A(T + 2)
                    if T == min(5, nt - 1):
                        precast_B()
                    g2 = stage2(T, st_ops)
            chk("A")
            phaseB(st_ops)
        except _Stop:
            pass
        S.barrier("sp", [o for o in S.ops if o.is_dma], "final")
        S.emit(nc, es)
    return nc


def make_in_maps(inputs, nt=32, ncores=NCORES):
    consts, _ = _consts(nt)
    f = lambda a: np.ascontiguousarray(np.asarray(a, dtype=np.float32))
    gT = lambda g: np.asarray(g, dtype=np.float32).reshape(8, 128).T
    shared = dict(
        w_in=f(inputs["w_in"][0]), w_att_up=f(inputs["w_att_up"][0]), w_ret_up=f(inputs["w_ret_up"][0]),
        w_out=f(inputs["w_out"][0]), w_ff1=f(inputs["w_ff1"][0]), w_ff2=f(inputs["w_ff2"][0]),
        gains=np.ascontiguousarray(np.concatenate([gT(inputs["norm_mix_gain"][0]), gT(inputs["ret_gn_gain"][0]),
                                                   gT(inputs["norm_mlp_gain"][0])], axis=1)),
        g_fin=f(inputs["norm_final_gain"]).reshape(1, D), b_gates=f(inputs["b_gates"][0]).reshape(1, 2 * D),
        sinks=f(inputs["attn_sinks"][0]).reshape(1, AH), **consts)
    x = np.asarray(inputs["x"], dtype=np.float32)
    return [dict(shared, x=np.ascontiguousarray(x[i, :nt * 128])) for i in range(ncores)]


def kernel(**inputs):
    nc = build_program(SEQ // 128)
    in_maps = make_in_maps(inputs)
    res = run_bass_kernel_spmd(nc, in_maps, core_ids=list(range(NCORES)))
    return np.stack([np.asarray(r["out"], dtype=np.float32) for r in res.results], axis=0)
```

```python
import os
import numpy as np
import ml_dtypes
from contextlib import ExitStack
import concourse.bass as bass
import concourse.mybir as mybir
from concourse.bass_utils import run_bass_kernel_spmd

F32 = mybir.dt.float32
BF16 = mybir.dt.bfloat16
AF = mybir.ActivationFunctionType
ALU = mybir.AluOpType

D = 1024
NCORES = 8
SEQ = 4096
AH, AKV, ADH = 8, 2, 64
RH, RDK, RDV = 4, 128, 256
DFF = 4096
INW = 5888
OFF_QA, OFF_KA, OFF_VA, OFF_QR, OFF_KR, OFF_VR, OFF_GR, OFF_GA, OFF_GRT = 0, 512, 640, 768, 1280, 1792, 2816, 3840, 4864
EPS = 1e-6


class Op:
    __slots__ = ("eng", "fn", "deps", "signal", "sig_idx", "dma_sem", "dma_val", "is_dma", "waits", "name", "preds", "idx", "dur")

    def __init__(self, eng, fn, name=""):
        self.eng = eng
        self.fn = fn
        self.deps = []
        self.signal = False
        self.sig_idx = None
        self.dma_sem = None
        self.dma_val = None
        self.is_dma = False
        self.waits = []
        self.name = name
        self.preds = []
        self.idx = 0
        self.dur = 0.0


class Sched:
    ENGS = ("pe", "act", "dve", "pool", "sp")

    def __init__(self, same_engine_sync=True):
        self.ops = []
        self.last_writer = {}
        self.readers = {}
        self.dma_counts = {}
        self.same_engine_sync = same_engine_sync
        self.last_on = {}
        self.thr = {}
        self.MAXQ = 4
        self.do_reorder = True
        self.fences = []

    def _deps(self, op, reads, writes):
        psr = [r for r in reads if isinstance(r, tuple) and r[0] == "ps"]
        if psr:
            reads = [r for r in reads if r not in psr]
            writes = list(writes) + [r for r in psr if r not in writes]
        deps = []
        for r in reads:
            w = self.last_writer.get(r)
            if w is not None:
                deps.append(w)
        for w_ in writes:
            w = self.last_writer.get(w_)
            if w is not None:
                deps.append(w)
            deps.extend(self.readers.get(w_, ()))
        for r in reads:
            self.readers.setdefault(r, []).append(op)
        for w_ in writes:
            self.last_writer[w_] = op
            self.readers[w_] = []
        seen = set()
        for d in deps:
            if d is op or id(d) in seen:
                continue
            seen.add(id(d))
            op.preds.append(d)
            if d.eng == op.eng and not d.is_dma:
                if op.eng == "pe" or not self.same_engine_sync:
                    continue
            op.deps.append(d)

    def add(self, eng, fn, reads=(), writes=(), name=""):
        op = Op(eng, fn, name)
        self._deps(op, reads, writes)
        self.ops.append(op)
        self.last_on[eng] = op
        return op

    def dma(self, eng, fn, semkey, reads=(), writes=(), name="", throttle=None):
        op = Op(eng, fn, name)
        op.is_dma = True
        n = self.dma_counts.get(semkey, 0) + 1
        self.dma_counts[semkey] = n
        op.dma_sem = semkey
        op.dma_val = 16 * n
        self._deps(op, reads, writes)
        tcls = "pool" if eng == "pool" else throttle
        if tcls is not None:
            lst = self.thr.setdefault(tcls, [])
            lst.append(op)
            if len(lst) > self.MAXQ:
                op.deps.append(lst[-1 - self.MAXQ])
                op.preds.append(lst[-1 - self.MAXQ])
        self.ops.append(op)
        return op

    def barrier(self, eng, deps, name="barrier"):
        op = Op(eng, None, name)
        op.deps = [d for d in deps if d is not None]
        op.preds = list(op.deps)
        self.ops.append(op)
        return op

    def fence_all(self, extra=()):
        self.fences.append(len(self.ops))
        for e in self.ENGS:
            op = self.barrier(e, list(extra), name="fence")

    def _patch_fence(self, before, fence_ops):
        last = {}
        for o in before:
            if o.fn is not None and not o.is_dma:
                last[o.eng] = o
        for f in fence_ops:
            extra = [d for d in f.deps if d.is_dma]
            f.deps = extra + [d for e, d in last.items() if e != f.eng]
            f.preds = list(f.deps)

    def _est(self, op):
        if op.fn is None:
            return 0.05
        n = getattr(op.fn, "n", 256)
        if op.is_dma:
            return 2.0 + n * 128 * 4 / 150e3
        if op.eng == "pe":
            return 0.04 + 0.000315 * max(n, 64)
        if op.eng == "act":
            return 0.2 + n * 0.95e-3
        if op.eng == "dve":
            return 0.12 + n * 1.1e-3
        if op.eng == "pool":
            return 0.35 + n * 2.0e-3
        return 0.05

    def reorder(self, window=128, slack=0.05, segments=None):
        window = int(os.environ.get("SCH_W", window))
        slack = float(os.environ.get("SCH_S", slack))
        xlat = float(os.environ.get("SCH_L", 0.25))
        ops = self.ops
        for i, op in enumerate(ops):
            op.idx = i
            op.dur = self._est(op)
        bounds = [0] + sorted(segments or []) + [len(ops)]
        new_ops = []
        finish = {}
        eng_free = {e: 0.0 for e in self.ENGS}
        for si in range(len(bounds) - 1):
            seg = ops[bounds[si]:bounds[si + 1]]
            if si > 0:
                self._patch_fence(new_ops, [o for o in seg if o.name == "fence"])
            queues = {e: [o for o in seg if o.eng == e] for e in self.ENGS}
            heads = {e: 0 for e in self.ENGS}
            done = set(id(o) for o in new_ops)
            nleft = len(seg)
            taken = set()
            while nleft:
                best = None
                for e in self.ENGS:
                    q = queues[e]
                    h = heads[e]
                    while h < len(q) and id(q[h]) in taken:
                        h += 1
                    heads[e] = h
                    cnt = 0
                    i = h
                    emin = None
                    cands = []
                    while i < len(q) and cnt < window:
                        o = q[i]
                        i += 1
                        if id(o) in taken:
                            continue
                        cnt += 1
                        ok = True
                        rdy = 0.0
                        for p in o.preds:
                            if id(p) not in done:
                                ok = False
                                break
                            f = finish[id(p)] + ((0.0 if e == "pe" else 0.12) if p.eng == e and not p.is_dma else xlat)
                            if f > rdy:
                                rdy = f
                        if not ok:
                            if e == "sp" or o.fn is None:
                                break
                            continue
                        st = max(rdy, eng_free[e])
                        cands.append((st, o))
                        if emin is None or st < emin:
                            emin = st
                        if e == "sp":
                            break
                    if not cands:
                        continue
                    pick = min((c for c in cands if c[0] <= emin + slack), key=lambda c: c[1].idx)
                    if best is None or pick[0] < best[0] or (pick[0] == best[0] and pick[1].idx < best[1].idx):
                        best = pick
                assert best is not None, "scheduler stuck"
                st, o = best
                taken.add(id(o))
                done.add(id(o))
                finish[id(o)] = st + o.dur
                if o.is_dma:
                    eng_free[o.eng] = st + 0.1
                else:
                    eng_free[o.eng] = st + o.dur
                new_ops.append(o)
                nleft -= 1
            if os.environ.get("VERB"):
                print("segment", si, "est finish us", max(finish.values()) if finish else 0.0, "ops", len(seg),
                      {e: round(sum(o.dur for o in seg if o.eng == e and not o.is_dma)) for e in self.ENGS})
        self.ops = new_ops
        self.est_total = max(finish.values()) if finish else 0.0

    def resolve(self):
        for op in self.ops:
            for d in op.deps:
                if not d.is_dma:
                    d.signal = True
        cnt = {e: 0 for e in self.ENGS}
        for op in self.ops:
            if op.signal:
                cnt[op.eng] += 1
                op.sig_idx = cnt[op.eng]
        known = {e: {} for e in self.ENGS}
        for op in self.ops:
            need = {}
            for d in op.deps:
                if d.is_dma:
                    key = ("dma", d.dma_sem)
                    val = d.dma_val
                else:
                    key = ("eng", d.eng)
                    val = d.sig_idx
                if need.get(key, 0) < val:
                    need[key] = val
            k = known[op.eng]
            for key, val in need.items():
                if k.get(key, 0) >= val:
                    continue
                k[key] = val
                op.waits.append((key, val))

    def emit(self, nc, es):
        if self.do_reorder:
            self.reorder(segments=self.fences)
        else:
            for fi in self.fences:
                self._patch_fence(self.ops[:fi], [o for o in self.ops[fi:] if o.name == "fence"])
        self.resolve()
        sems = {}
        for e in self.ENGS:
            sems[("eng", e)] = es.enter_context(nc.semaphore("s_" + e))
        for i, k in enumerate(self.dma_counts):
            sems[("dma", k)] = es.enter_context(nc.semaphore("d%d" % i))
        block = es.enter_context(nc.Block())
        per = {e: [o for o in self.ops if o.eng == e] for e in self.ENGS}

        def run(engh, e):
            for op in per[e]:
                for key, val in op.waits:
                    engh.wait_ge(sems[key], val)
                if op.fn is None:
                    continue
                ins = op.fn(engh)
                if op.is_dma:
                    ins.then_inc(sems[("dma", op.dma_sem)], 16)
                elif op.signal:
                    ins.then_inc(sems[("eng", e)], 1)

        @block.tensor
        def _(t):
            run(t, "pe")

        @block.scalar
        def _(t):
            run(t, "act")

        @block.vector
        def _(t):
            run(t, "dve")

        @block.gpsimd
        def _(t):
            run(t, "pool")

        @block.sync
        def _(t):
            run(t, "sp")


def _fsz(ap):
    sh = ap.shape
    n = 1
    for d in sh[1:]:
        n *= int(d)
    return n


def OPC(name, *a, **k):
    fn = lambda e: getattr(e, name)(*a, **k)
    try:
        if name == "matmul":
            fn.n = _fsz(k["rhs"])
        elif name == "transpose":
            fn.n = 128
        elif name == "dma_start":
            fn.n = _fsz(k["out"])
        else:
            fn.n = _fsz(k["out"] if "out" in k else a[0])
    except Exception:
        fn.n = 256
    fn.opname = name
    return fn


class DB:
    def __init__(self, name, views):
        self.name, self.v = name, views

    def __call__(self, T):
        return self.v[T % len(self.v)]

    def k(self, T):
        return (self.name, T % len(self.v))


class Arena:
    def __init__(self, ap, n):
        self.ap = ap
        self.n = n
        self.off = 0
        self.hi = 0

    def alloc(self, shape, dt):
        nel = int(np.prod(shape))
        nw = nel if dt == F32 else (nel + 1) // 2
        nw = (nw + 7) // 8 * 8
        assert self.off + nw <= self.n, "SBUF arena overflow: need %d have %d" % (self.off + nw, self.n)
        v = self.ap[:, self.off:self.off + nw]
        self.off += nw
        self.hi = max(self.hi, self.off)
        if dt != F32:
            v = v.bitcast(dt)
        v = v[:, 0:nel]
        if len(shape) == 2:
            v = v.rearrange("p (a b) -> p a b", a=shape[0])
        elif len(shape) == 3:
            v = v.rearrange("p (a b c) -> p a b c", a=shape[0], b=shape[1])
        return v


def _consts(nt):
    s = nt * 128
    def _tables(dim, theta):
        try:
            import jax
            import jax.numpy as jnp
            with jax.default_device(jax.devices("cpu")[0]):
                pos = jnp.arange(s, dtype=jnp.float32)
                inv = theta ** (-jnp.arange(0, dim, 2, dtype=jnp.float32) / dim)
                ang = pos[:, None] * inv[None, :]
                c, sn = np.asarray(jnp.cos(ang), dtype=np.float32), np.asarray(jnp.sin(ang), dtype=np.float32)
            assert c.shape == (s, dim // 2) and np.isfinite(c).all() and np.isfinite(sn).all()
            return c, sn
        except Exception:
            pos = np.arange(s, dtype=np.float32)
            inv = (np.float32(theta) ** (-np.arange(0, dim, 2, dtype=np.float32) / np.float32(dim))).astype(np.float32)
            ang = (pos[:, None] * inv[None, :]).astype(np.float32)
            return np.cos(ang).astype(np.float32), np.sin(ang).astype(np.float32)

    ca, sa = _tables(16, 500000.0)
    cca = np.concatenate([ca, ca], -1)
    ssa = np.concatenate([-sa, sa], -1)
    cr, sr = _tables(128, 10000.0)
    ropeR = np.concatenate([cr, cr, -sr, sr, cca, ssa], -1).astype(np.float32)
    h = np.arange(RH, dtype=np.float64)
    lg = np.log1p(-np.exp2(-5.0 - h))
    idx = np.arange(128, dtype=np.float64)
    scale = RDK ** -0.5
    mr = np.exp(-(idx[:, None, None] + 1.0) * lg[None, :, None]) * scale * (idx[:, None, None] <= idx[None, None, :])
    kdec = np.exp((127.0 - idx)[:, None] * lg[None, :]) * scale
    eps4 = 4.0 * 1e-6 * np.exp(-2.0 * (idx[:, None] + 1.0) * lg[None, :])
    cdec = np.exp(128.0 * lg)
    k_ = np.arange(128)[:, None]
    q_ = np.arange(128)[None, :]
    maskA = np.stack([(k_ <= q_), (k_ > q_)], 1).astype(np.float32)
    small = np.concatenate([kdec, eps4], -1).astype(np.float32)
    return dict(
        ident=np.eye(128, dtype=np.float32).astype(ml_dtypes.bfloat16),
        ropeR=np.ascontiguousarray(ropeR),
        maskR=np.ascontiguousarray(mr.reshape(128, 512)).astype(np.float32),
        maskA=np.ascontiguousarray(maskA.reshape(128, 256)).astype(ml_dtypes.bfloat16),
        small=small,
    ), [float(c) for c in cdec]


class _Stop(Exception):
    pass


def build_program(nt=32, taps=(), stop=None):
    s = nt * 128
    _, cdec = _consts(1)
    nc = bass.Bass("TRN2", target_bir_lowering=False)

    def din(name, shape, dt=F32):
        return nc.dram_tensor(name, list(shape), dt, kind="ExternalInput").ap()

    x_d = din("x", [s, D])
    win_d = din("w_in", [D, INW])
    wau_d = din("w_att_up", [512, D])
    wru_d = din("w_ret_up", [D, D])
    wo_d = din("w_out", [D, D])
    w1_d = din("w_ff1", [D, DFF])
    w2_d = din("w_ff2", [DFF, D])
    gains_d = din("gains", [128, 24])
    gfin_d = din("g_fin", [1, D])
    bg_d = din("b_gates", [1, 2 * D])
    sinks_d = din("sinks", [1, AH])
    ident_d = din("ident", [128, 128], BF16)
    ropeR_d = din("ropeR", [s, 288])
    maskR_d = din("maskR", [128, 512])
    maskA_d = din("maskA", [128, 256], BF16)
    small_d = din("small", [128, 8])
    out_d = nc.dram_tensor("out", [s, D], F32, kind="ExternalOutput").ap()
    import os
    mT_d = nc.dram_tensor("mT_scratch", [nt, 128, D], BF16, kind=os.environ.get("MTKIND", "Internal")).ap()
    wo_b = nc.dram_tensor("wo_bf16", [D, D], BF16, kind="Internal").ap()
    w1_b = nc.dram_tensor("w1_bf16", [D, DFF], BF16, kind="Internal").ap()
    w2_b = nc.dram_tensor("w2_bf16", [DFF, D], BF16, kind="Internal").ap()
    tap_d = {}
    for name, shape in taps:
        tap_d[name] = nc.dram_tensor("tap_" + name, [nt, 128] + list(shape), F32, kind="ExternalOutput").ap()

    S = Sched()
    NARENA = 53120
    with ExitStack() as es:
        arena_t = es.enter_context(nc.sbuf_tensor("arena", [128, NARENA], F32))
        psum_t = es.enter_context(nc.psum_tensor("psum", [128, 4096], F32))
        AR = Arena(arena_t, NARENA)

        def bank(i):
            return psum_t[:, i * 512:(i + 1) * 512]

        def bank_bf(i):
            return psum_t[:, i * 512:(i + 1) * 512].bitcast(BF16).rearrange("p (c n) -> p c n", c=8)

        bank_rr = [0]
        NF = 6
        tr_rr = [0]

        def next_bank():
            b = bank_rr[0]
            bank_rr[0] = (b + 1) % NF
            return b

        def next_tr():
            b = 6 + tr_rr[0]
            tr_rr[0] = (tr_rr[0] + 1) % 2
            return b

        ident = AR.alloc([128], BF16)
        gains = AR.alloc([24], F32)
        small = AR.alloc([8], F32)
        mhalf = AR.alloc([8], F32)
        rstd = AR.alloc([2], F32)
        ssq = AR.alloc([2], F32)
        S.dma("sp", OPC("dma_start", out=ident, in_=ident_d), "c_id", writes=["ident"])
        S.dma("sp", OPC("dma_start", out=gains, in_=gains_d), "c_g", writes=["gains"])
        S.dma("sp", OPC("dma_start", out=small, in_=small_d), "c_sm", writes=["small"])
        S.add("pool", OPC("memset", mhalf, -0.5), writes=["mhalf"])
        common_off = AR.off

        def gbc(i, n=128):
            return gains[:, i * 8:(i + 1) * 8].unsqueeze(2).to_broadcast([128, 8, n])

        def rmsnorm_T(src, srckey, dst_bf, dstkey, par, gi, xs, xskey):
            sq, rs = ssq[:, par:par + 1], rstd[:, par:par + 1]
            S.add("act", OPC("activation", out=xs, in_=src, func=AF.Square, accum_out=sq),
                  reads=[srckey], writes=[xskey, ("ssq", par)])
            S.add("dve", OPC("tensor_scalar", out=rs, in0=sq, scalar1=1.0 / D, scalar2=EPS, op0=ALU.mult, op1=ALU.add),
                  reads=[("ssq", par)], writes=[("rstd", par)])
            S.add("pool", OPC("tensor_tensor", out=rs, in0=rs, in1=mhalf[:, 0:1], op=ALU.pow),
                  reads=[("rstd", par), "mhalf"], writes=[("rstd", par)])
            S.add("act", OPC("activation", out=xs, in_=src, func=AF.Copy, scale=rs),
                  reads=[srckey, ("rstd", par)], writes=[xskey])
            tb = next_tr()
            for c in range(8):
                S.add("pe", OPC("transpose", out=bank_bf(tb)[:, c, :], in_=xs[:, c * 128:(c + 1) * 128], identity=ident),
                      reads=[xskey, "ident"], writes=[("ps", tb)])
            S.add("dve", OPC("tensor_tensor", out=dst_bf, in0=bank_bf(tb), in1=gbc(gi), op=ALU.mult),
                  reads=[("ps", tb), "gains"], writes=[dstkey])

        def tap(name, T, src, key):
            if name in tap_d:
                S.dma("sp", OPC("dma_start", out=tap_d[name][T], in_=src), ("tap", name), reads=[key])

        Win = AR.alloc([8, INW], BF16)
        Wau = AR.alloc([4, D], BF16)
        Wru = AR.alloc([8, D], BF16)
        maskR = AR.alloc([4, 128], F32)
        maskA = AR.alloc([2, 128], BF16)
        esink = AR.alloc([8], F32)
        bhl = AR.alloc([2 * D], BF16)
        y_sb = AR.alloc([D], F32)
        ones1 = AR.alloc([128], BF16)
        xt = DB("xt", [AR.alloc([D], F32) for _ in range(2)])
        rR = DB("rR", [AR.alloc([288], F32) for _ in range(2)])
        xs = AR.alloc([D], BF16)
        xnT = DB("xnT", [AR.alloc([8, 128], BF16) for _ in range(2)])
        qa_sb = AR.alloc([512], BF16)
        ka_sb = AR.alloc([128], BF16)
        va_aug = DB("va", [AR.alloc([2, 66], BF16) for _ in range(2)])
        rtmpA = AR.alloc([10, 16], F32)
        rtmpB = AR.alloc([10, 16], F32)
        rt12 = AR.alloc([D], F32)
        rt1, rt2 = rt12[:, 0:512], rt12[:, 512:1024]
        qr_sb = AR.alloc([512], BF16)
        kr_sb = AR.alloc([512], BF16)
        kd_sb = DB("kd", [AR.alloc([4, 128], BF16) for _ in range(2)])
        vr_sb = DB("vr", [AR.alloc([D], BF16) for _ in range(2)])
        tg = AR.alloc([512], F32)
        sg = DB("sg", [AR.alloc([D], F32) for _ in range(2)])
        qaT = DB("qaT", [AR.alloc([4, 128], BF16) for _ in range(2)])
        kaT = DB("kaT", [AR.alloc([128], BF16) for _ in range(2)])
        qkT = DB("qkT", [AR.alloc([8, 128], BF16) for _ in range(2)])
        p_sb = AR.alloc([4, 512], BF16)
        den = AR.alloc([8], F32)
        att_sb = AR.alloc([512], BF16)
        attT = AR.alloc([4, 128], BF16)
        am_sb = AR.alloc([4, 128], BF16)
        S32 = AR.alloc([4, 256], F32)
        Sbf = DB("Sbf", [AR.alloc([4, 256], BF16) for _ in range(2)])
        gst = AR.alloc([4, 6], F32)
        gmv = AR.alloc([4, 2], F32)
        grs = AR.alloc([4], F32)
        gated = AR.alloc([D], BF16)
        gatedT = AR.alloc([8, 128], BF16)
        tgab = [(AR.alloc([512], F32), AR.alloc([512], F32)) for _ in range(2)]
        merged = AR.alloc([D], BF16)
        mergedT = DB("mT", [AR.alloc([8, 128], BF16) for _ in range(1)])
        phaseA_hi = AR.off
        if os.environ.get('VERB'):
            print('phaseA words', phaseA_hi, 'of', NARENA)

        for T0 in range(min(2, nt)):
            S.dma("sp", OPC("dma_start", out=xt(T0), in_=x_d[T0 * 128:(T0 + 1) * 128, :]), ("x", T0 % 2), writes=[xt.k(T0)])
            S.dma("sp", OPC("dma_start", out=rR(T0), in_=ropeR_d[T0 * 128:(T0 + 1) * 128, :]), ("rR", T0 % 2), writes=[rR.k(T0)])
        S.dma("sp", OPC("dma_start", out=maskR, in_=maskR_d.rearrange("p (h q) -> p h q", h=4)), "c_mr", writes=["maskR"])
        S.dma("sp", OPC("dma_start", out=maskA, in_=maskA_d.rearrange("p (b q) -> p b q", b=2)), "c_ma", writes=["maskA"])
        S.dma("sp", OPC("dma_start", out=esink, in_=sinks_d.partition_broadcast(128)), "c_sk", writes=["esink"])
        S.add("act", OPC("activation", out=esink, in_=esink, func=AF.Exp), reads=["esink"], writes=["esink"])
        S.add("pool", OPC("memset", ones1, 1.0), writes=["ones1"])
        for p_ in range(2):
            S.add("pool", OPC("memset", va_aug(p_), 1.0), writes=[va_aug.k(p_)])
        S.add("dve", OPC("memset", bhl[0:64, :], 0.0), writes=["bhl"])
        for hb in range(2):
            cs_ = slice(hb * D, (hb + 1) * D)
            S.dma("sp", OPC("dma_start", out=rt12[0:1, :], in_=bg_d[:, cs_]), "c_bg", writes=["rt1", "rt2"])
            S.dma("sp", OPC("dma_start", out=rt12[32:33, :], in_=bg_d[:, cs_]), "c_bg2", writes=["rt1", "rt2"])
            S.add("dve", OPC("tensor_copy", out=bhl[0:1, cs_], in_=rt12[0:1, :]), reads=["rt1", "rt2"], writes=["bhl"])
            S.add("dve", OPC("tensor_copy", out=xs[32:33, :], in_=rt12[32:33, :]), reads=["rt1", "rt2"], writes=["xs"])
            S.add("dve", OPC("tensor_tensor", out=bhl[32:33, cs_], in0=rt12[32:33, :], in1=xs[32:33, :], op=ALU.subtract),
                  reads=["rt1", "rt2", "xs"], writes=["bhl"])

        def wload(dst, src, key):
            sk = os.environ.get("SKIPW", "")
            if sk == "1" or key in sk.split(",") or (sk.startswith("only:") and key not in sk[5:].split(",")):
                return
            S.dma("pool", OPC("dma_start", out=dst, in_=src), ("w", key), writes=[("w", key)])

        win_v = win_d.rearrange("(c p) n -> p c n", p=128)
        for j in range(4):
            for a in range(2):
                hh = a * 4 + j
                wload(Win[:, :, j * 128 + a * 64: j * 128 + a * 64 + 64], win_v[:, :, hh * 64:(hh + 1) * 64], "qa")
        groups = [("kava", OFF_KA, 256), ("qr", OFF_QR, 512), ("kr", OFF_KR, 512), ("vr0", OFF_VR, 512),
                  ("vr1", OFF_VR + 512, 512), ("gr0", OFF_GR, 512), ("gr1", OFF_GR + 512, 512)]
        for key, c0, ncol in groups:
            wload(Win[:, :, c0:c0 + ncol], win_v[:, :, c0:c0 + ncol], key)
        for key, c0 in (("ga0", OFF_GA), ("ga1", OFF_GA + 512), ("gt0", OFF_GRT), ("gt1", OFF_GRT + 512)):
            wload(Win[:, :, c0:c0 + 512], win_v[:, :, c0:c0 + 512], key)
        wload(Wau, wau_d.rearrange("(c p) n -> p c n", p=128), "wau")
        wload(Wru[:, 0:4, :], wru_d.rearrange("(c p) n -> p c n", p=128)[:, 0:4, :], "wru0")
        wload(Wru[:, 4:8, :], wru_d.rearrange("(c p) n -> p c n", p=128)[:, 4:8, :], "wru1")

        wo_v = wo_d.rearrange("(c p) n -> p c n", p=128)
        w1_v = w1_d.rearrange("(c p) n -> p c n", p=128)
        w2_v = w2_d.rearrange("(c p) n -> p c n", p=128)
        wo_bv = wo_b.rearrange("(c p) n -> p c n", p=128)
        w1_bv = w1_b.rearrange("(c p) n -> p c n", p=128)
        w2_bv = w2_b.rearrange("(c p) n -> p c n", p=128)
        def precast_B():
            for n in range(2):
                S.dma("pool", OPC("dma_start", out=wo_bv[:, :, n * 512:(n + 1) * 512], in_=wo_v[:, :, n * 512:(n + 1) * 512]),
                      ("cb", "wo%d" % n), writes=[("wbs", "wo%d" % n)])
            for j in range(8):
                S.dma("pool", OPC("dma_start", out=w1_bv[:, :, j * 512:(j + 1) * 512], in_=w1_v[:, :, j * 512:(j + 1) * 512]),
                      ("cb", "w1_%d" % j), writes=[("wbs", "w1_%d" % j)])
            for j in range(8):
                S.dma("pool", OPC("dma_start", out=w2_bv[:, j * 4:(j + 1) * 4, :], in_=w2_v[:, j * 4:(j + 1) * 4, :]),
                      ("cb", "w2_%d" % j), writes=[("wbs", "w2_%d" % j)])

        def load_A(T):
            par = T % 2
            S.dma("sp", OPC("dma_start", out=xt(T), in_=x_d[T * 128:(T + 1) * 128, :]), ("x", par), writes=[xt.k(T)])
            S.dma("sp", OPC("dma_start", out=rR(T), in_=ropeR_d[T * 128:(T + 1) * 128, :]), rR.k(T), writes=[rR.k(T)])

        def proj(T, par, c0, ncol, wkeys, bias_c0=None):
            b = next_bank()
            for c in range(8):
                S.add("pe", OPC("matmul", bank(b)[:, 0:ncol], lhsT=xnT(T)[:, c, :], rhs=Win[:, c, c0:c0 + ncol],
                                                    start=(c == 0), stop=(c == 7 and bias_c0 is None)),
                      reads=[xnT.k(T)] + [("w", k) for k in wkeys], writes=[("ps", b)])
            if bias_c0 is not None:
                S.add("pe", OPC("matmul", bank(b)[:, 0:ncol], lhsT=ones1[0:33, :], rhs=bhl[0:33, bias_c0:bias_c0 + ncol], start=False, stop=True),
                      reads=["ones1", "bhl"], writes=[("ps", b)])
            return b

        def rope_small(psv, nh, dst, dstkey, T, b):
            cc = rR(T)[:, 256:272].unsqueeze(1).to_broadcast([128, nh, 16])
            sn = rR(T)[:, 272:280].unsqueeze(1).to_broadcast([128, nh, 8])
            sp_ = rR(T)[:, 280:288].unsqueeze(1).to_broadcast([128, nh, 8])
            ta_, tb_ = rtmpA[:, 0:nh, :], rtmpB[:, 0:nh, :]
            S.add("dve", OPC("tensor_tensor", out=ta_, in0=psv[:, :, 0:16], in1=cc, op=ALU.mult),
                  reads=[("ps", b), rR.k(T)], writes=["rtmpA"])
            S.add("dve", OPC("tensor_tensor", out=tb_[:, :, 0:8], in0=psv[:, :, 8:16], in1=sn, op=ALU.mult),
                  reads=[("ps", b), rR.k(T)], writes=["rtmpB"])
            S.add("dve", OPC("tensor_tensor", out=tb_[:, :, 8:16], in0=psv[:, :, 0:8], in1=sp_, op=ALU.mult),
                  reads=[("ps", b), rR.k(T)], writes=["rtmpB"])
            S.add("pool", OPC("tensor_tensor", out=dst[:, :, 0:16], in0=ta_, in1=tb_, op=ALU.add),
                  reads=["rtmpA", "rtmpB"], writes=[dstkey])

        def rope_big(b, T):
            cc = rR(T)[:, 0:128].unsqueeze(1).to_broadcast([128, 4, 128])
            sn = rR(T)[:, 128:192].unsqueeze(1).to_broadcast([128, 4, 64])
            sp_ = rR(T)[:, 192:256].unsqueeze(1).to_broadcast([128, 4, 64])
            r1 = rt1.rearrange("p (h d) -> p h d", h=4)
            r2 = rt2.rearrange("p (h d) -> p h d", h=4)
            S.add("act", OPC("activation", out=rt1, in_=bank(b), func=AF.Copy), reads=[("ps", b)], writes=["rt1"])
            S.add("dve", OPC("tensor_tensor", out=r2[:, :, 0:64], in0=r1[:, :, 64:128], in1=sn, op=ALU.mult),
                  reads=["rt1", rR.k(T)], writes=["rt2"])
            S.add("dve", OPC("tensor_tensor", out=r2[:, :, 64:128], in0=r1[:, :, 0:64], in1=sp_, op=ALU.mult),
                  reads=["rt1", rR.k(T)], writes=["rt2"])
            S.add("dve", OPC("tensor_tensor", out=r1, in0=r1, in1=cc, op=ALU.mult),
                  reads=["rt1", rR.k(T)], writes=["rt1"])

        def stage1(T):
            par = T % 2
            rmsnorm_T(xt(T), xt.k(T), xnT(T), xnT.k(T), par, 0, xs, "xs")
            yield
            b = proj(T, par, OFF_QA, 512, ["qa"])
            S.add("act", OPC("activation", out=qa_sb, in_=bank(b), func=AF.Copy), reads=[("ps", b)], writes=["qa_sb"])
            rope_small(bank(b).rearrange("p (h d) -> p h d", h=8), 8, qa_sb.rearrange("p (h d) -> p h d", h=8), "qa_sb", T, b)
            yield
            b = proj(T, par, OFF_KA, 256, ["kava"])
            S.add("act", OPC("activation", out=ka_sb, in_=bank(b)[:, 0:128], func=AF.Copy), reads=[("ps", b)], writes=["ka_sb"])
            S.add("act", OPC("activation", out=va_aug(T)[:, :, 0:64], in_=bank(b)[:, 128:256].rearrange("p (g d) -> p g d", g=2),
                                                func=AF.Copy), reads=[("ps", b)], writes=[va_aug.k(T)])
            rope_small(bank(b)[:, 0:128].rearrange("p (h d) -> p h d", h=2), 2, ka_sb.rearrange("p (h d) -> p h d", h=2), "ka_sb", T, b)
            yield
            b = proj(T, par, OFF_QR, 512, ["qr"])
            rope_big(b, T)
            S.add("pool", OPC("tensor_tensor", out=qr_sb, in0=rt1, in1=rt2, op=ALU.add), reads=["rt1", "rt2"], writes=["qr_sb"])
            yield
            b = proj(T, par, OFF_KR, 512, ["kr"])
            rope_big(b, T)
            S.add("pool", OPC("tensor_tensor", out=rt1, in0=rt1, in1=rt2, op=ALU.add), reads=["rt1", "rt2"], writes=["rt1"])
            S.add("act", OPC("activation", out=kr_sb, in_=rt1, func=AF.Copy), reads=["rt1"], writes=["kr_sb"])
            S.add("pool", OPC("tensor_tensor", out=kd_sb(T), in0=rt1.rearrange("p (h d) -> p h d", h=4),
                                                    in1=small[:, 0:4].unsqueeze(2).to_broadcast([128, 4, 128]), op=ALU.mult),
                  reads=["rt1", "small"], writes=[kd_sb.k(T)])
            yield
            for i in range(2):
                b = proj(T, par, OFF_VR + i * 512, 512, ["vr%d" % i])
                S.add("act", OPC("activation", out=vr_sb(T)[:, i * 512:(i + 1) * 512], in_=bank(b), func=AF.Copy),
                      reads=[("ps", b)], writes=[vr_sb.k(T)])
            yield
            for i in range(2):
                b = proj(T, par, OFF_GR + i * 512, 512, ["gr%d" % i])
                S.add("act", OPC("activation", out=tg, in_=bank(b), func=AF.Tanh, scale=0.5), reads=[("ps", b)], writes=["tg"])
                S.add("dve", OPC("scalar_tensor_tensor", out=sg(T)[:, i * 512:(i + 1) * 512], in0=tg, scalar=1.0, in1=bank(b),
                                                                        op0=ALU.add, op1=ALU.mult),
                      reads=["tg", ("ps", b)], writes=[sg.k(T)])
            yield
            tb = next_tr()
            for j in range(4):
                S.add("pe", OPC("transpose", out=bank_bf(tb)[:, j, :], in_=qa_sb[:, j * 128:(j + 1) * 128], identity=ident),
                      reads=["qa_sb", "ident"], writes=[("ps", tb)])
            S.add("pe", OPC("transpose", out=bank_bf(tb)[:, 4, :], in_=ka_sb, identity=ident),
                  reads=["ka_sb", "ident"], writes=[("ps", tb)])
            S.add("act", OPC("activation", out=qaT(T), in_=bank_bf(tb)[:, 0:4, :], func=AF.Copy), reads=[("ps", tb)], writes=[qaT.k(T)])
            S.add("dve", OPC("tensor_copy", out=kaT(T), in_=bank_bf(tb)[:, 4, :]), reads=[("ps", tb)], writes=[kaT.k(T)])
            tb2 = next_tr()
            for hh in range(4):
                S.add("pe", OPC("transpose", out=bank_bf(tb2)[:, hh, :], in_=qr_sb[:, hh * 128:(hh + 1) * 128], identity=ident),
                      reads=["qr_sb", "ident"], writes=[("ps", tb2)])
            for hh in range(4):
                S.add("pe", OPC("transpose", out=bank_bf(tb2)[:, 4 + hh, :], in_=kr_sb[:, hh * 128:(hh + 1) * 128], identity=ident),
                      reads=["kr_sb", "ident"], writes=[("ps", tb2)])
            S.add("act", OPC("activation", out=qkT(T), in_=bank_bf(tb2), func=AF.Copy), reads=[("ps", tb2)], writes=[qkT.k(T)])

        def stage2(T, st_ops):
            par = T % 2
            blks = [(0, T)] + ([(1, T - 1)] if T > 0 else [])
            for g in range(2):
                for (bi, bp) in blks:
                    b = next_bank()
                    idx = g * 2 + bi
                    S.add("pe", OPC("matmul", bank(b), lhsT=kaT(bp)[g * 64:(g + 1) * 64, :], rhs=qaT(T)[g * 64:(g + 1) * 64, :, :],
                                    start=True, stop=True), reads=[kaT.k(bp), qaT.k(T)], writes=[("ps", b)])
                    S.add("act", OPC("activation", out=p_sb[:, idx, :], in_=bank(b), func=AF.Exp, scale=ADH ** -0.5),
                          reads=[("ps", b)], writes=[("p", idx)])
                    pv4 = p_sb[:, idx, :].rearrange("p (j q) -> p j q", j=4)
                    S.add("pool", OPC("tensor_tensor", out=pv4, in0=pv4, in1=maskA[:, bi:bi + 1, :].to_broadcast([128, 4, 128]), op=ALU.mult),
                          reads=[("p", idx), "maskA"], writes=[("p", idx)])
            b = next_bank()
            for h in range(4):
                S.add("pe", OPC("matmul", bank(b)[:, h * 128:(h + 1) * 128], lhsT=qkT(T)[:, 4 + h, :], rhs=qkT(T)[:, h, :], start=True, stop=True),
                      reads=[qkT.k(T)], writes=[("ps", b)])
            S.add("dve", OPC("tensor_tensor", out=am_sb, in0=bank(b).rearrange("p (h q) -> p h q", h=4), in1=maskR, op=ALU.mult),
                  reads=[("ps", b), "maskR"], writes=["am_sb"])
            yield
            pvb = [next_bank(), next_bank()]
            for h in range(8):
                g, j = h // 4, h % 4
                for n_, (bi, bp) in enumerate(blks):
                    idx = g * 2 + bi
                    S.add("pe", OPC("matmul", bank(pvb[g])[:, j * 65:(j + 1) * 65], lhsT=p_sb[:, idx, j * 128:(j + 1) * 128],
                                    rhs=va_aug(bp)[:, g, 0:65], start=(n_ == 0), stop=(n_ == len(blks) - 1)),
                          reads=[("p", idx), va_aug.k(bp)], writes=[("ps", pvb[g])])
            yb = [next_bank(), next_bank()]
            for h in range(4):
                reg = bank(yb[h // 2])[:, (h % 2) * 256:(h % 2 + 1) * 256]
                S.add("pe", OPC("matmul", reg, lhsT=am_sb[:, h, :], rhs=vr_sb(T)[:, h * 256:(h + 1) * 256], start=True, stop=(T == 0)),
                      reads=["am_sb", vr_sb.k(T)], writes=[("ps", yb[h // 2])])
                if T > 0:
                    S.add("pe", OPC("matmul", reg, lhsT=qkT(T)[:, h, :], rhs=Sbf(T - 1)[:, h, :], start=False, stop=True),
                          reads=[qkT.k(T), Sbf.k(T - 1)], writes=[("ps", yb[h // 2])])
            for g in range(2):
                pv = bank(pvb[g])[:, 0:260].rearrange("p (j d) -> p j d", j=4)
                dn = den[:, g * 4:(g + 1) * 4]
                S.add("dve", OPC("tensor_tensor", out=dn.unsqueeze(2), in0=pv[:, :, 64:65], in1=esink[:, g * 4:(g + 1) * 4].unsqueeze(2), op=ALU.add),
                      reads=[("ps", pvb[g]), "esink"], writes=[("den", g)])
                S.add("dve", OPC("reciprocal", out=dn, in_=dn), reads=[("den", g)], writes=[("den", g)])
                S.add("dve", OPC("tensor_tensor", out=att_sb[:, g * 256:(g + 1) * 256].rearrange("p (j d) -> p j d", j=4), in0=pv[:, :, 0:64],
                                 in1=dn.unsqueeze(2).to_broadcast([128, 4, 64]), op=ALU.mult),
                      reads=[("ps", pvb[g]), ("den", g)], writes=[("att_sb", g)])
            yield
            for h in range(4):
                reg = bank(yb[h // 2])[:, (h % 2) * 256:(h % 2 + 1) * 256]
                S.add("dve", OPC("bn_stats", out=gst[:, h, :], in_=reg), reads=[("ps", yb[h // 2])], writes=[("gst", h)])
                if h % 2 == 1:
                    i_ = h // 2
                    S.add("act", OPC("activation", out=y_sb[:, i_ * 512:(i_ + 1) * 512], in_=bank(yb[i_]), func=AF.Copy),
                          reads=[("ps", yb[i_])], writes=[("y_sb", i_)])
            for h in range(4):
                S.add("dve", OPC("bn_aggr", out=gmv[:, h, :], in_=gst[:, h, :]), reads=[("gst", h)], writes=[("gmv", h)])
            S.add("dve", OPC("scalar_tensor_tensor", out=grs.unsqueeze(2), in0=gmv[:, :, 1:2], scalar=4.0, in1=small[:, 4:8].unsqueeze(2),
                             op0=ALU.mult, op1=ALU.add), reads=[("gmv", 0), ("gmv", 1), ("gmv", 2), ("gmv", 3), "small"], writes=["grs"])
            S.add("pool", OPC("tensor_tensor", out=grs, in0=grs, in1=mhalf[:, 0:4], op=ALU.pow), reads=["grs", "mhalf"], writes=["grs"])
            S.add("pool", OPC("tensor_tensor", out=sg(T).rearrange("p (h d) -> p h d", h=4), in0=sg(T).rearrange("p (h d) -> p h d", h=4),
                              in1=grs.unsqueeze(2).to_broadcast([128, 4, 256]), op=ALU.mult), reads=[sg.k(T), "grs"], writes=[sg.k(T)])
            if T < nt - 1:
                kb = [next_bank(), next_bank()]
                for h in range(4):
                    reg = bank(kb[h // 2])[:, (h % 2) * 256:(h % 2 + 1) * 256]
                    S.add("pe", OPC("matmul", reg, lhsT=kd_sb(T)[:, h, :], rhs=vr_sb(T)[:, h * 256:(h + 1) * 256], start=True, stop=True),
                          reads=[kd_sb.k(T), vr_sb.k(T)], writes=[("ps", kb[h // 2])])
            tb = next_tr()
            for c in range(4):
                S.add("pe", OPC("transpose", out=bank_bf(tb)[:, c, :], in_=att_sb[:, c * 128:(c + 1) * 128], identity=ident),
                      reads=[("att_sb", c // 2), "ident"], writes=[("ps", tb)])
            S.add("act", OPC("activation", out=attT, in_=bank_bf(tb)[:, 0:4, :], func=AF.Copy), reads=[("ps", tb)], writes=["attT"])
            for h in range(4):
                S.add("dve", OPC("scalar_tensor_tensor", out=gated[:, h * 256:(h + 1) * 256], in0=y_sb[:, h * 256:(h + 1) * 256], scalar=gmv[:, h, 0:1],
                                 in1=sg(T)[:, h * 256:(h + 1) * 256], op0=ALU.subtract, op1=ALU.mult),
                      reads=[("y_sb", h // 2), ("gmv", h), sg.k(T)], writes=[("gated", h)])
            if T < nt - 1:
                for h in range(4):
                    reg = bank(kb[h // 2])[:, (h % 2) * 256:(h % 2 + 1) * 256]
                    if T == 0:
                        S.add("act", OPC("activation", out=S32[:, h, :], in_=reg, func=AF.Copy), reads=[("ps", kb[h // 2])], writes=[("S32", h)])
                    else:
                        S.add("dve", OPC("scalar_tensor_tensor", out=S32[:, h, :], in0=S32[:, h, :], scalar=cdec[h], in1=reg,
                                         op0=ALU.mult, op1=ALU.add), reads=[("ps", kb[h // 2]), ("S32", h)], writes=[("S32", h)])
                S.add("act", OPC("activation", out=Sbf(T), in_=S32, func=AF.Copy), reads=[("S32", 0), ("S32", 1), ("S32", 2), ("S32", 3)], writes=[Sbf.k(T)])
            yield
            for n in range(2):
                bga = proj(T, par, OFF_GA + n * 512, 512, ["ga%d" % n], bias_c0=n * 512)
                bgr = proj(T, par, OFF_GRT + n * 512, 512, ["gt%d" % n], bias_c0=D + n * 512)
                tga, tgr = tgab[n]
                S.add("act", OPC("activation", out=tga, in_=bank(bga), func=AF.Tanh, scale=0.5), reads=[("ps", bga)], writes=[("tga", n)])
                S.add("act", OPC("activation", out=tgr, in_=bank(bgr), func=AF.Tanh, scale=0.5), reads=[("ps", bgr)], writes=[("tgr", n)])
                if n == 0:
                    tb = next_tr()
                    for c in range(8):
                        S.add("pe", OPC("transpose", out=bank_bf(tb)[:, c, :], in_=gated[:, c * 128:(c + 1) * 128], identity=ident),
                              reads=[("gated", c // 2), "ident"], writes=[("ps", tb)])
                    S.add("dve", OPC("tensor_tensor", out=gatedT, in0=bank_bf(tb), in1=gbc(1), op=ALU.mult),
                          reads=[("ps", tb), "gains"], writes=["gatedT"])
                    yield
            for n in range(2):
                tga, tgr = tgab[n]
                ba = next_bank()
                for c in range(4):
                    S.add("pe", OPC("matmul", bank(ba), lhsT=attT[:, c, :], rhs=Wau[:, c, n * 512:(n + 1) * 512], start=(c == 0), stop=(c == 3)),
                          reads=["attT", ("w", "wau")], writes=[("ps", ba)])
                br = next_bank()
                for c in range(8):
                    S.add("pe", OPC("matmul", bank(br), lhsT=gatedT[:, c, :], rhs=Wru[:, c, n * 512:(n + 1) * 512], start=(c == 0), stop=(c == 7)),
                          reads=["gatedT", ("w", "wru0"), ("w", "wru1")], writes=[("ps", br)])
                S.add("dve", OPC("scalar_tensor_tensor", out=tga, in0=tga, scalar=1.0, in1=bank(ba), op0=ALU.add, op1=ALU.mult),
                      reads=[("tga", n), ("ps", ba)], writes=[("tga", n)])
                S.add("dve", OPC("scalar_tensor_tensor", out=tgr, in0=tgr, scalar=1.0, in1=bank(br), op0=ALU.add, op1=ALU.mult),
                      reads=[("tgr", n), ("ps", br)], writes=[("tgr", n)])
                S.add("pool", OPC("tensor_tensor", out=merged[:, n * 512:(n + 1) * 512], in0=tga, in1=tgr, op=ALU.add),
                      reads=[("tga", n), ("tgr", n)], writes=[("merged", n)])
            yield
            tb = next_tr()
            for c in range(8):
                S.add("pe", OPC("transpose", out=bank_bf(tb)[:, c, :], in_=merged[:, c * 128:(c + 1) * 128], identity=ident),
                      reads=[("merged", c // 4), "ident"], writes=[("ps", tb)])
            S.add("act", OPC("activation", out=mergedT(T), in_=bank_bf(tb), func=AF.Copy), reads=[("ps", tb)], writes=[mergedT.k(T)])
            st_ops.append(S.dma("sp", OPC("dma_start", out=mT_d[T], in_=mergedT(T).rearrange("p c n -> p (c n)")), ("mTst", par),
                                reads=[mergedT.k(T)]))

        def phaseB(st_ops):
            S.fence_all(extra=st_ops[-2:])
            AR.off = common_off
            Wo = AR.alloc([8, D], BF16)
            W1 = AR.alloc([8, DFF], BF16)
            W2 = AR.alloc([32, D], BF16)
            gfin = AR.alloc([D], F32)
            xb = DB("xb", [AR.alloc([D], F32) for _ in range(2)])
            mTb = DB("mTb", [AR.alloc([8, 128], BF16) for _ in range(2)])
            h1 = DB("h1", [AR.alloc([D], F32) for _ in range(2)])
            junk = AR.alloc([D], BF16)
            h1s = AR.alloc([D], BF16)
            h1nT = AR.alloc([8, 128], BF16)
            rr = [AR.alloc([512], F32) for _ in range(2)]
            hT = AR.alloc([32, 128], BF16)
            hsq = [AR.alloc([512], BF16) for _ in range(2)]
            ob = DB("ob", [AR.alloc([D], F32) for _ in range(2)])

            S.dma("sp", OPC("dma_start", out=gfin, in_=gfin_d.partition_broadcast(128)), "c_gf", writes=["gfin"])
            def wloadB(part):
                for n in (range(2) if part == 0 else ()):
                    S.dma("sp", OPC("dma_start", out=Wo[:, :, n * 512:(n + 1) * 512], in_=wo_bv[:, :, n * 512:(n + 1) * 512]),
                          ("wb", "wo%d" % n), reads=[("wbs", "wo%d" % n)], writes=[("w", "wo%d" % n)], throttle="wB")
                for j in (range(8) if part == 1 else ()):
                    S.dma("sp", OPC("dma_start", out=W1[:, :, j * 512:(j + 1) * 512], in_=w1_bv[:, :, j * 512:(j + 1) * 512]),
                          ("wb", "w1_%d" % j), reads=[("wbs", "w1_%d" % j)], writes=[("w", "w1_%d" % j)], throttle="wB")
                for j in (range(8) if part == 1 else ()):
                    S.dma("sp", OPC("dma_start", out=W2[:, j * 4:(j + 1) * 4, :], in_=w2_bv[:, j * 4:(j + 1) * 4, :]),
                          ("wb", "w2_%d" % j), reads=[("wbs", "w2_%d" % j)], writes=[("w", "w2_%d" % j)], throttle="wB")

            def load_B(T):
                par = T % 2
                S.dma("sp", OPC("dma_start", out=xb(T), in_=x_d[T * 128:(T + 1) * 128, :]), xb.k(T), writes=[xb.k(T)])
                S.dma("sp", OPC("dma_start", out=mTb(T), in_=mT_d[T].rearrange("p (c n) -> p c n", c=8)), mTb.k(T),
                      writes=[mTb.k(T)])

            def p1(T):
                par = T % 2
                for n in range(2):
                    b = next_bank()
                    for c in range(8):
                        S.add("pe", OPC("matmul", bank(b), lhsT=mTb(T)[:, c, :], rhs=Wo[:, c, n * 512:(n + 1) * 512], start=(c == 0), stop=(c == 7)),
                              reads=[mTb.k(T), ("w", "wo%d" % n)], writes=[("ps", b)])
                    S.add("dve", OPC("scalar_tensor_tensor", out=h1(T)[:, n * 512:(n + 1) * 512], in0=bank(b), scalar=0.5,
                                     in1=xb(T)[:, n * 512:(n + 1) * 512], op0=ALU.mult, op1=ALU.add),
                          reads=[("ps", b), xb.k(T)], writes=[h1.k(T)])
                sq, rs = ssq[:, 0:1], rstd[:, 0:1]
                S.add("act", OPC("activation", out=h1s, in_=h1(T), func=AF.Square, accum_out=sq), reads=[h1.k(T)], writes=["h1s", ("ssq", 0)])
                S.add("dve", OPC("tensor_scalar", out=rs, in0=sq, scalar1=1.0 / D, scalar2=EPS, op0=ALU.mult, op1=ALU.add),
                      reads=[("ssq", 0)], writes=[("rstd", 0)])
                S.add("pool", OPC("tensor_tensor", out=rs, in0=rs, in1=mhalf[:, 0:1], op=ALU.pow), reads=[("rstd", 0), "mhalf"], writes=[("rstd", 0)])
                S.add("act", OPC("activation", out=h1s, in_=h1(T), func=AF.Copy, scale=rs), reads=[h1.k(T), ("rstd", 0)], writes=["h1s"])

            def p2(T):
                tb = next_tr()
                for c in range(8):
                    S.add("pe", OPC("transpose", out=bank_bf(tb)[:, c, :], in_=h1s[:, c * 128:(c + 1) * 128], identity=ident),
                          reads=["h1s", "ident"], writes=[("ps", tb)])
                S.add("dve", OPC("tensor_tensor", out=h1nT, in0=bank_bf(tb), in1=gbc(2), op=ALU.mult),
                      reads=[("ps", tb), "gains"], writes=["h1nT"])

            def ff1(T):
                for jb in range(8):
                    b = next_bank()
                    for c in range(8):
                        S.add("pe", OPC("matmul", bank(b), lhsT=h1nT[:, c, :], rhs=W1[:, c, jb * 512:(jb + 1) * 512], start=(c == 0), stop=(c == 7)),
                              reads=["h1nT", ("w", "w1_%d" % jb)], writes=[("ps", b)])
                    r_ = rr[jb % 2]
                    hq = hsq[jb % 2]
                    S.add("act", OPC("activation", out=r_, in_=bank(b), func=AF.Relu), reads=[("ps", b)], writes=[("rr", jb % 2)])
                    S.add("dve", OPC("tensor_tensor", out=hq, in0=r_, in1=r_, op=ALU.mult), reads=[("rr", jb % 2)], writes=[("hsq", jb % 2)])
                    tb = next_tr()
                    for jj in range(4):
                        S.add("pe", OPC("transpose", out=bank_bf(tb)[:, jj, :], in_=hq[:, jj * 128:(jj + 1) * 128], identity=ident),
                              reads=[("hsq", jb % 2), "ident"], writes=[("ps", tb)])
                    S.add("act", OPC("activation", out=hT[:, jb * 4:(jb + 1) * 4, :], in_=bank_bf(tb)[:, 0:4, :], func=AF.Copy),
                          reads=[("ps", tb)], writes=[("hT", jb)])

            def ff2(T):
                par = T % 2
                o_ = ob(T)
                for n in range(2):
                    b = next_bank()
                    for k in range(32):
                        S.add("pe", OPC("matmul", bank(b), lhsT=hT[:, k, :], rhs=W2[:, k, n * 512:(n + 1) * 512], start=(k == 0), stop=(k == 31)),
                              reads=[("hT", k // 4), ("w", "w2_%d" % (k // 4))], writes=[("ps", b)])
                    S.add("dve", OPC("tensor_tensor", out=o_[:, n * 512:(n + 1) * 512], in0=bank(b), in1=h1(T)[:, n * 512:(n + 1) * 512], op=ALU.add),
                          reads=[("ps", b), h1.k(T)], writes=[ob.k(T)])
                sq, rs = ssq[:, 1:2], rstd[:, 1:2]
                S.add("act", OPC("activation", out=junk, in_=o_, func=AF.Square, accum_out=sq), reads=[ob.k(T)], writes=["junk", ("ssq", 1)])
                S.add("dve", OPC("tensor_scalar", out=rs, in0=sq, scalar1=1.0 / D, scalar2=EPS, op0=ALU.mult, op1=ALU.add),
                      reads=[("ssq", 1)], writes=[("rstd", 1)])
                S.add("pool", OPC("tensor_tensor", out=rs, in0=rs, in1=mhalf[:, 0:1], op=ALU.pow), reads=[("rstd", 1), "mhalf"], writes=[("rstd", 1)])
                S.add("dve", OPC("scalar_tensor_tensor", out=o_, in0=o_, scalar=rs, in1=gfin, op0=ALU.mult, op1=ALU.mult),
                      reads=[ob.k(T), ("rstd", 1), "gfin"], writes=[ob.k(T)])
                return S.dma("sp", OPC("dma_start", out=out_d[T * 128:(T + 1) * 128, :], in_=o_), ("ost", par), reads=[ob.k(T)])

            load_B(0)
            wloadB(0)
            if nt > 1:
                load_B(1)
            wloadB(1)
            p1(0)
            p2(0)
            for T in range(nt):
                if T + 1 < nt:
                    p1(T + 1)
                ff1(T)
                if T + 1 < nt:
                    p2(T + 1)
                ff2(T)
                if T + 2 < nt:
                    load_B(T + 2)

        def chk(tag):
            if stop == tag:
                raise _Stop()

        try:
            chk("consts")
            st_ops = []
            g2 = None
            for T in range(nt + 1):
                g1 = stage1(T) if T < nt else None
                while g1 is not None or g2 is not None:
                    if g1 is not None:
                        try:
                            next(g1)
                        except StopIteration:
                            g1 = None
                    if g2 is not None:
                        try:
                            next(g2)
                        except StopIteration:
                            g2 = None
                if T < nt:
                    if T + 2 < nt:
                        load_A(T + 2)
                    if T == min(5, nt - 1):
                        precast_B()
                    g2 = stage2(T, st_ops)
            chk("A")
            phaseB(st_ops)
        except _Stop:
            pass
        S.barrier("sp", [o for o in S.ops if o.is_dma], "final")
        S.emit(nc, es)
    return nc


def make_in_maps(inputs, nt=32, ncores=NCORES):
    consts, _ = _consts(nt)
    f = lambda a: np.ascontiguousarray(np.asarray(a, dtype=np.float32))
    gT = lambda g: np.asarray(g, dtype=np.float32).reshape(8, 128).T
    shared = dict(
        w_in=f(inputs["w_in"][0]), w_att_up=f(inputs["w_att_up"][0]), w_ret_up=f(inputs["w_ret_up"][0]),
        w_out=f(inputs["w_out"][0]), w_ff1=f(inputs["w_ff1"][0]), w_ff2=f(inputs["w_ff2"][0]),
        gains=np.ascontiguousarray(np.concatenate([gT(inputs["norm_mix_gain"][0]), gT(inputs["ret_gn_gain"][0]),
                                                   gT(inputs["norm_mlp_gain"][0])], axis=1)),
        g_fin=f(inputs["norm_final_gain"]).reshape(1, D), b_gates=f(inputs["b_gates"][0]).reshape(1, 2 * D),
        sinks=f(inputs["attn_sinks"][0]).reshape(1, AH), **consts)
    x = np.asarray(inputs["x"], dtype=np.float32)
    return [dict(shared, x=np.ascontiguousarray(x[i, :nt * 128])) for i in range(ncores)]


def kernel(**inputs):
    nc = build_program(SEQ // 128)
    in_maps = make_in_maps(inputs)
    res = run_bass_kernel_spmd(nc, in_maps, core_ids=list(range(NCORES)))
    return np.stack([np.asarray(r["out"], dtype=np.float32) for r in res.results], axis=0)
```
